# Optimizing a Trainium2 kernel written in Bass

```python
import math
import jax, jax.numpy as jnp
from jax import lax
import numpy as np

D_MODEL = 2048
BATCH = 8
SEQ = 2048
DEPTH = 2

HEAD_DIM = 64
GROUP_WIDTH = D_MODEL // 4
D_MIX = 4 * GROUP_WIDTH
H_A = GROUP_WIDTH // HEAD_DIM
G_A = 2
CMP_BLK = 32
CMP_STRIDE = 16
CMP_HIDDEN = 128
SLC_BLK = 64
N_SEL = 8
WIN_A = 512
H_B = GROUP_WIDTH // HEAD_DIM
G_B = 2
WIN_B = 128
CONV_CH = GROUP_WIDTH
CONV_W = 3
H_D = GROUP_WIDTH // HEAD_DIM
DILATED = ((128, 1), (512, 4), (2048, 16))
NUM_BUCKETS = 32
MAX_DISTANCE = 2048
N_BIAS_HEADS = H_A + H_B + H_D
BLOCK = 128
RMS_EPS = 1e-6
NEG = -1e30
FORCE = 1e4

SEG_SIZES = (
    H_A * HEAD_DIM,
    G_A * HEAD_DIM, G_A * HEAD_DIM,
    G_A * HEAD_DIM, G_A * HEAD_DIM,
    G_A * HEAD_DIM, G_A * HEAD_DIM,
    3 * H_A,
    GROUP_WIDTH,
    H_B * HEAD_DIM, G_B * HEAD_DIM, G_B * HEAD_DIM, GROUP_WIDTH,
    CONV_CH, CONV_CH, CONV_CH, GROUP_WIDTH,
    H_D * HEAD_DIM, H_D * HEAD_DIM, H_D * HEAD_DIM, GROUP_WIDTH,
)
D_IN = sum(SEG_SIZES)

kernel_name = "hybrid_nsa_swa_conv_dilated_block"


def rms_norm(x, w):
    xf = x.astype(jnp.float32)
    y = xf * lax.rsqrt(jnp.mean(xf * xf, axis=-1, keepdims=True) + RMS_EPS)
    return (y * w.astype(jnp.float32)).astype(x.dtype)


def t5_bucket(dist):
    exact = NUM_BUCKETS // 2
    d = jnp.maximum(dist, 0)
    large = exact + (jnp.log(jnp.maximum(d, exact).astype(jnp.float32) / exact)
                     / math.log(MAX_DISTANCE / exact) * (NUM_BUCKETS - exact)).astype(jnp.int32)
    return jnp.where(d < exact, d, jnp.minimum(large, NUM_BUCKETS - 1))


def masked_softmax(s, mask):
    s = jnp.where(mask, s, NEG)
    e = jnp.where(mask, jnp.exp(s - jnp.max(s, axis=-1, keepdims=True)), 0.0)
    return e / jnp.maximum(jnp.sum(e, axis=-1, keepdims=True), 1e-30)


def to_heads(t, n):
    b, s, _ = t.shape
    return t.reshape(b, s, n, HEAD_DIM).transpose(0, 2, 1, 3)


def from_heads(t):
    b, n, s, hd = t.shape
    return t.transpose(0, 2, 1, 3).reshape(b, s, n * hd)


def banded_attention(q, k, v, max_dist, bias_tab, dist_scale=1, sink=None):
    b, h, s, hd = q.shape
    hkv = k.shape[1]
    r = h // hkv
    nprev = -(-max_dist // BLOCK)
    nb = -(-s // BLOCK)
    sp = nb * BLOCK
    pad = sp - s
    qp = jnp.pad(q, ((0, 0), (0, 0), (0, pad), (0, 0)))
    kp = jnp.pad(k, ((0, 0), (0, 0), (nprev * BLOCK, pad), (0, 0))).reshape(b, hkv, nb + nprev, BLOCK, hd)
    vp = jnp.pad(v, ((0, 0), (0, 0), (nprev * BLOCK, pad), (0, 0))).reshape(b, hkv, nb + nprev, BLOCK, hd)
    kw = jnp.concatenate([kp[:, :, j:j + nb] for j in range(nprev + 1)], axis=3)
    vw = jnp.concatenate([vp[:, :, j:j + nb] for j in range(nprev + 1)], axis=3)
    wlen = (nprev + 1) * BLOCK
    qb = qp.reshape(b, hkv, r, nb, BLOCK, hd)
    sc = jnp.einsum('bgrcqd,bgckd->bgrcqk', qb, kw).astype(jnp.float32) * (hd ** -0.5)
    qi = jnp.arange(BLOCK)[:, None]
    kj = jnp.arange(wlen)[None, :]
    dist = qi - kj + nprev * BLOCK
    kpos = jnp.arange(nb)[:, None, None] * BLOCK + kj[None] - nprev * BLOCK
    mask = (dist >= 0)[None] & (dist <= max_dist)[None] & (kpos >= 0)
    bias = bias_tab[t5_bucket(dist * dist_scale)].astype(jnp.float32)
    bias = bias.transpose(2, 0, 1).reshape(hkv, r, BLOCK, wlen)[None, :, :, None]
    sc = jnp.where(mask, sc + bias, NEG)
    lse = jax.nn.logsumexp(sc, axis=-1)
    if sink is not None:
        lse = jnp.logaddexp(lse, sink.astype(jnp.float32).reshape(hkv, r)[None, :, :, None, None])
    p = jnp.exp(sc - lse[..., None])
    o = jnp.einsum('bgrcqk,bgckd->bgrcqd', p, vw).reshape(b, h, sp, hd)[:, :, :s]
    return o, lse.reshape(b, h, sp)[:, :, :s]


def nsa_mixer(q, kc, vc, ks, vs, kw, vw, gates, cmp_pos, cmp_w1, cmp_w2, bias_tab):
    b, h, s, hd = q.shape
    g = kc.shape[1]
    r = h // g
    scale = hd ** -0.5
    t_pos = jnp.arange(s)
    qg = q.reshape(b, g, r, s, hd)

    n_cmp = (s - CMP_BLK) // CMP_STRIDE + 1
    cmp_start = jnp.arange(n_cmp) * CMP_STRIDE
    tok = cmp_start[:, None] + jnp.arange(CMP_BLK)[None]

    def compress(t, pos, w1, w2):
        blocks = t[:, :, tok] + pos
        flat = blocks.reshape(b, g, n_cmp, CMP_BLK * hd)
        return jax.nn.silu(flat @ w1) @ w2

    kcmp = compress(kc, cmp_pos[0], cmp_w1[0], cmp_w2[0])
    vcmp = compress(vc, cmp_pos[1], cmp_w1[1], cmp_w2[1])
    dist_c = t_pos[:, None] - (cmp_start + CMP_BLK - 1)[None]
    bias_c = bias_tab[t5_bucket(dist_c)].astype(jnp.float32).transpose(2, 0, 1).reshape(g, r, s, n_cmp)
    sc = jnp.einsum('bgrsd,bgcd->bgrsc', qg, kcmp).astype(jnp.float32) * scale + bias_c
    p_cmp = masked_softmax(sc, dist_c >= 0)
    o_cmp = jnp.einsum('bgrsc,bgcd->bgrsd', p_cmp, vcmp).reshape(b, h, s, hd)

    n_slc = s // SLC_BLK
    n_sel = min(N_SEL, n_slc)
    slc_start = jnp.arange(n_slc) * SLC_BLK
    overlap = ((cmp_start[:, None] < slc_start[None] + SLC_BLK)
               & (cmp_start[:, None] + CMP_BLK > slc_start[None])).astype(jnp.float32)
    imp = jnp.einsum('bgrsc,cn->bgsn', p_cmp, overlap)
    cur = (t_pos // SLC_BLK)[:, None]
    blk = jnp.arange(n_slc)[None]
    forced = (blk == 0) | (blk == cur) | (blk == cur - 1)
    imp = jnp.where(blk > cur, NEG, jnp.where(forced, FORCE, imp))
    _, sel = lax.top_k(imp, n_sel)
    ksb = ks.reshape(b, g, n_slc, SLC_BLK, hd)
    vsb = vs.reshape(b, g, n_slc, SLC_BLK, hd)
    tab_g = bias_tab.reshape(NUM_BUCKETS, g, r).transpose(1, 0, 2)
    gather = jax.vmap(jax.vmap(lambda blocks, ids: blocks[ids]))
    n_tok = n_sel * SLC_BLK

    def sel_chunk(args):
        qc, idc, tq = args
        nq = tq.shape[0]
        kg = gather(ksb, idc).reshape(b, g, nq, n_tok, hd)
        vg = gather(vsb, idc).reshape(b, g, nq, n_tok, hd)
        kpos = (idc[..., None] * SLC_BLK + jnp.arange(SLC_BLK)).reshape(b, g, nq, n_tok)
        dist = tq[:, None] - kpos
        bias = jax.vmap(lambda tb, bk: tb[bk], in_axes=(0, 1), out_axes=1)(tab_g, t5_bucket(dist))
        bias = jnp.moveaxis(bias, -1, 2).astype(jnp.float32)
        sc_s = jnp.einsum('bgrqd,bgqtd->bgrqt', qc, kg).astype(jnp.float32) * scale + bias
        p = masked_softmax(sc_s, (dist >= 0)[:, :, None])
        return jnp.einsum('bgrqt,bgqtd->bgrqd', p, vg)

    nchunk = s // BLOCK
    q_ch = jnp.moveaxis(qg.reshape(b, g, r, nchunk, BLOCK, hd), 3, 0)
    id_ch = jnp.moveaxis(sel.reshape(b, g, nchunk, BLOCK, n_sel), 2, 0)
    t_ch = t_pos.reshape(nchunk, BLOCK)
    o_slc = lax.map(sel_chunk, (q_ch, id_ch, t_ch))
    o_slc = jnp.moveaxis(o_slc, 0, 3).reshape(b, h, s, hd)

    o_win, _ = banded_attention(q, kw, vw, WIN_A - 1, bias_tab)

    gt = jax.nn.sigmoid(gates.astype(jnp.float32)).reshape(b, s, 3, h).transpose(2, 0, 3, 1)[..., None]
    return gt[0] * o_cmp + gt[1] * o_slc + gt[2] * o_win


def short_conv_mixer(bg, cg, hx, conv_w):
    u = cg * hx
    y = lax.conv_general_dilated(u, conv_w[:, None, :].astype(u.dtype), window_strides=(1,),
                                 padding=[(CONV_W - 1, 0)], dimension_numbers=('NWC', 'WIO', 'NWC'),
                                 feature_group_count=u.shape[-1])
    return bg * y


def dilated_mixer(q, k, v, bias_tab):
    b, h, s, hd = q.shape
    outs, lses = [], []
    for window, dil in DILATED:
        def split(t):
            return t.reshape(b, h, s // dil, dil, hd).transpose(0, 3, 1, 2, 4).reshape(b * dil, h, s // dil, hd)
        o, lse = banded_attention(split(q), split(k), split(v), window // dil, bias_tab, dist_scale=dil)
        outs.append(o.reshape(b, dil, h, s // dil, hd).transpose(0, 2, 3, 1, 4).reshape(b, h, s, hd))
        lses.append(lse.reshape(b, dil, h, s // dil).transpose(0, 2, 3, 1).reshape(b, h, s))
    wts = jax.nn.softmax(jnp.stack(lses), axis=0)
    return jnp.einsum('pbhs,pbhsd->bhsd', wts, jnp.stack(outs))


def hybrid_layer(x, norm_w, w_in, w_out, conv_w, sinks, cmp_pos, cmp_w1, cmp_w2, rel_bias):
    hn = rms_norm(x, norm_w)
    proj = hn @ w_in
    offs = np.cumsum(SEG_SIZES)[:-1].tolist()
    (aq, akc, avc, aks, avs, akw, avw, agates, agate,
     bq, bk, bv, bgate, cb, cc, ch, cgate, dq, dk, dv, dgate) = jnp.split(proj, offs, axis=-1)

    o_a = nsa_mixer(to_heads(aq, H_A), to_heads(akc, G_A), to_heads(avc, G_A), to_heads(aks, G_A),
                    to_heads(avs, G_A), to_heads(akw, G_A), to_heads(avw, G_A), agates,
                    cmp_pos, cmp_w1, cmp_w2, rel_bias[:, :H_A])
    o_b, _ = banded_attention(to_heads(bq, H_B), to_heads(bk, G_B), to_heads(bv, G_B), WIN_B - 1,
                              rel_bias[:, H_A:H_A + H_B], sink=sinks)
    o_c = short_conv_mixer(cb, cc, ch, conv_w)
    o_d = dilated_mixer(to_heads(dq, H_D), to_heads(dk, H_D), to_heads(dv, H_D), rel_bias[:, H_A + H_B:])

    mix = jnp.concatenate([
        from_heads(o_a).astype(x.dtype) * jax.nn.silu(agate),
        from_heads(o_b).astype(x.dtype) * jax.nn.silu(bgate),
        o_c.astype(x.dtype) * jax.nn.silu(cgate),
        from_heads(o_d).astype(x.dtype) * jax.nn.silu(dgate),
    ], axis=-1)
    return x + (mix @ w_out).astype(x.dtype)


def setup_inputs(seed: int = 0) -> dict:
    key = jax.random.key(seed)
    ks = jax.random.split(key, 12)
    f32 = jnp.float32
    x = jax.random.normal(ks[0], (BATCH, SEQ, D_MODEL), f32)
    norm_w = 1.0 + 0.02 * jax.random.normal(ks[1], (DEPTH, D_MODEL), f32)
    w_in = jax.random.normal(ks[2], (DEPTH, D_MODEL, D_IN), f32) * D_MODEL ** -0.5
    w_out = jax.random.normal(ks[3], (DEPTH, D_MIX, D_MODEL), f32) * D_MIX ** -0.5
    conv_w = jax.random.normal(ks[4], (DEPTH, CONV_W, CONV_CH), f32) * CONV_W ** -0.5
    sinks = 0.5 * jax.random.normal(ks[5], (DEPTH, H_B), f32)
    cmp_pos = 0.1 * jax.random.normal(ks[6], (DEPTH, 2, CMP_BLK, HEAD_DIM), f32)
    cmp_w1 = jax.random.normal(ks[7], (DEPTH, 2, CMP_BLK * HEAD_DIM, CMP_HIDDEN), f32) * (CMP_BLK * HEAD_DIM) ** -0.5
    cmp_w2 = jax.random.normal(ks[8], (DEPTH, 2, CMP_HIDDEN, HEAD_DIM), f32) * CMP_HIDDEN ** -0.5
    rel_bias = 0.5 * jax.random.normal(ks[9], (NUM_BUCKETS, N_BIAS_HEADS), f32)
    final_norm_w = 1.0 + 0.02 * jax.random.normal(ks[10], (D_MODEL,), f32)
    return {"x": x, "norm_w": norm_w, "w_in": w_in, "w_out": w_out, "conv_w": conv_w,
            "sinks": sinks, "cmp_pos": cmp_pos, "cmp_w1": cmp_w1, "cmp_w2": cmp_w2,
            "rel_bias": rel_bias, "final_norm_w": final_norm_w}


def reference(x, norm_w, w_in, w_out, conv_w, sinks, cmp_pos, cmp_w1, cmp_w2, rel_bias, final_norm_w):
    for layer in range(DEPTH):
        x = hybrid_layer(x, norm_w[layer], w_in[layer], w_out[layer], conv_w[layer], sinks[layer],
                         cmp_pos[layer], cmp_w1[layer], cmp_w2[layer], rel_bias)
    return rms_norm(x, final_norm_w)
```

```python
import numpy as np
import concourse.bass as bass
import concourse.mybir as mybir
from concourse.bass_utils import run_bass_kernel_spmd

F32 = mybir.dt.float32
BF16 = mybir.dt.bfloat16
AF = mybir.ActivationFunctionType
ALU = mybir.AluOpType

D_MODEL = 2048
SEQ = 2048
DEPTH = 2
D_IN = 7192
NT = 16
W = 2304
SAME_ENGINE_SYNC = True

SEG = dict(aq=0, akc=512, avc=640, aks=768, avs=896, akw=1024, avw=1152, agates=1280, agate=1304,
           bq=1816, bk=2328, bv=2456, bgate=2584,
           cb=3096, cc=3608, ch=4120, cgate=4632,
           dq=5144, dk=5656, dv=6168, dgate=6680)


class _Op:
    __slots__ = ("eng", "fn", "deps", "is_dma", "dkey", "dval", "sig", "sval", "idx")


class Sched:
    ENGS = ("pe", "act", "dve", "pool", "sp")

    def __init__(self, nc):
        self.nc = nc
        self.ops = []
        self.last_w = {}
        self.readers = {}
        self.dma_cnt = {}
        self.dma_last = {}
        self.last_eng = {}
        self.fence_deps = []
        self.fence_pending = set()

    def fence(self):
        self.fence_deps = sorted(set(list(self.last_eng.values()) + list(self.dma_last.values())))
        self.fence_pending = set(self.ENGS)

    def add(self, eng, fn, reads=(), writes=(), dkey=None):
        op = _Op()
        op.eng = eng
        op.fn = fn
        op.is_dma = dkey is not None
        op.dkey = dkey
        op.sig = False
        op.sval = 0
        op.dval = 0
        op.idx = len(self.ops)
        deps = {}
        for k in reads:
            w = self.last_w.get(k)
            if w is not None:
                deps[w] = True
        for k in writes:
            w = self.last_w.get(k)
            if w is not None:
                deps.setdefault(w, False)
            for r in self.readers.get(k, ()):
                deps.setdefault(r, False)
        if dkey is not None:
            prev = self.dma_last.get(dkey)
            if prev is not None:
                deps.setdefault(prev, False)
            self.dma_cnt[dkey] = self.dma_cnt.get(dkey, 0) + 1
            op.dval = 16 * self.dma_cnt[dkey]
            self.dma_last[dkey] = op.idx
        if eng in self.fence_pending:
            self.fence_pending.discard(eng)
            for d in self.fence_deps:
                deps.setdefault(d, False)
        deps.pop(op.idx, None)
        op.deps = sorted(deps.items())
        if dkey is None:
            self.last_eng[eng] = op.idx
        for k in reads:
            self.readers.setdefault(k, []).append(op.idx)
        for k in writes:
            self.last_w[k] = op.idx
            self.readers[k] = []
        self.ops.append(op)
        return op

    def mm(self, out, lhsT, rhs, start=True, stop=True, reads=(), writes=()):
        return self.add("pe", lambda e: e.matmul(out, lhsT, rhs, start=start, stop=stop), reads, writes)

    def tr(self, out, in_, ident, reads=(), writes=()):
        return self.add("pe", lambda e: e.transpose(out, in_, ident), reads, writes)

    def act(self, out, in_, func, reads=(), writes=(), eng="act", **kw):
        return self.add(eng, lambda e: e.activation(out, in_, func, **kw), reads, writes)

    def tt(self, eng, out, in0, in1, op, reads=(), writes=()):
        return self.add(eng, lambda e: e.tensor_tensor(out, in0, in1, op), reads, writes)

    def ts(self, eng, out, in0, s1, s2, op0, op1=None, reads=(), writes=()):
        if op1 is None:
            return self.add(eng, lambda e: e.tensor_scalar(out, in0, s1, None, op0), reads, writes)
        return self.add(eng, lambda e: e.tensor_scalar(out, in0, s1, s2, op0, op1), reads, writes)

    def stt(self, out, in0, scalar, in1, op0, op1, reads=(), writes=()):
        return self.add("dve", lambda e: e.scalar_tensor_tensor(out, in0, scalar, in1, op0, op1), reads, writes)

    def cp(self, eng, out, in_, reads=(), writes=()):
        if eng == "act":
            return self.add(eng, lambda e: e.copy(out, in_), reads, writes)
        return self.add(eng, lambda e: e.tensor_copy(out, in_), reads, writes)

    def memset(self, eng, ap, val, writes=()):
        return self.add(eng, lambda e: e.memset(ap, val), (), writes)

    def dma(self, eng, out, in_, dkey, reads=(), writes=(), **kw):
        return self.add(eng, lambda e: e.dma_start(out, in_, **kw), reads, writes, dkey=dkey)

    def emit(self, final_waits=()):
        nc = self.nc
        ops = self.ops
        def same_ok(y, op, raw):
            return y.eng == op.eng and (y.eng in ("pe", "sp") or not raw or not SAME_ENGINE_SYNC)

        for op in ops:
            for d, raw in op.deps:
                y = ops[d]
                if y.is_dma:
                    continue
                if not same_ok(y, op, raw):
                    y.sig = True
        cnt = {e: 0 for e in self.ENGS}
        for op in ops:
            if op.sig and not op.is_dma:
                cnt[op.eng] += 1
                op.sval = cnt[op.eng]
        esem = {e: nc.alloc_semaphore("s_" + e) for e in self.ENGS}
        dsem = {k: nc.alloc_semaphore("d_%s" % (k,)) for k in self.dma_cnt}
        per_eng = {e: [op for op in ops if op.eng == e] for e in self.ENGS}
        all_sems = list(esem.values()) + list(dsem.values())
        for sm in all_sems:
            nc.gpsimd.sem_clear(sm)
        nc.all_engine_barrier()

        def run(engname, eng):
            waited = {}
            for op in per_eng[engname]:
                need = {}
                for d, raw in op.deps:
                    y = ops[d]
                    if y.is_dma:
                        s, v = ("d", y.dkey), y.dval
                    elif same_ok(y, op, raw):
                        continue
                    else:
                        s, v = ("e", y.eng), y.sval
                    if need.get(s, 0) < v:
                        need[s] = v
                for s, v in need.items():
                    if waited.get(s, 0) >= v:
                        continue
                    waited[s] = v
                    eng.wait_ge(dsem[s[1]] if s[0] == "d" else esem[s[1]], v)
                ins = op.fn(eng)
                if op.is_dma:
                    ins.then_inc(dsem[op.dkey], 16)
                elif op.sig:
                    ins.then_inc(esem[op.eng], 1)
            if engname == "sp":
                for k in final_waits:
                    eng.wait_ge(dsem[k], 16 * self.dma_cnt[k])

        with nc.Block() as block:
            @block.tensor
            def _(e):
                run("pe", e)

            @block.scalar
            def _(e):
                run("act", e)

            @block.vector
            def _(e):
                run("dve", e)

            @block.gpsimd
            def _(e):
                run("pool", e)

            @block.sync
            def _(e):
                run("sp", e)

        nc.all_engine_barrier()
        for sm in all_sems:
            nc.gpsimd.sem_clear(sm)
        nc.all_engine_barrier()


def _t5_bucket_np(d):
    d = np.maximum(d, 0)
    dl = np.maximum(d, 16).astype(np.float32)
    large = 16 + (np.log(dl / np.float32(16)) / np.float32(np.log(2048 / 16)) * np.float32(16)).astype(np.int32)
    return np.where(d < 16, d, np.minimum(large, 31))


def host_constants():
    c = {}
    c["ident"] = np.eye(128, dtype=np.float32)
    idx = np.arange(W)
    d = idx - 127
    bk = _t5_bucket_np(d)
    oh = np.zeros((32, W), np.float32)
    oh[bk, idx] = 1.0
    oh[:, d < 0] = 0.0
    c["onehot"] = oh
    mv = np.zeros((4, W), np.float32)
    nn = d >= 0
    mv[0] = nn
    mv[1] = nn & (d <= 511)
    mv[2] = nn & (d <= 127)
    mv[3] = (nn & (d <= 128)).astype(np.float32) + (nn & (d % 4 == 0) & (d <= 512)) + (nn & (d % 16 == 0) & (d <= 2047))
    c["multv"] = np.ascontiguousarray(np.broadcast_to(mv[None], (128, 4, W))).astype(np.float32)
    t = np.arange(SEQ)
    cur = (t // 64)[:, None]
    blk = np.arange(32)[None]
    forced = (blk == 0) | (blk == cur) | (blk == cur - 1)
    am = np.where((blk > cur) | forced, 0.0, 1.0).astype(np.float32)
    bm = np.where(blk > cur, -1e30, np.where(forced, 1e4, 0.0)).astype(np.float32)
    c["amask"] = am.reshape(16, 128, 32).transpose(1, 0, 2).copy()
    c["bmask"] = bm.reshape(16, 128, 32).transpose(1, 0, 2).copy()
    cs = np.arange(127) * 16
    ss = np.arange(32) * 64
    ov = ((cs[:, None] < ss[None] + 64) & (cs[:, None] + 32 > ss[None])).astype(np.float32)
    c["ovl1"] = np.concatenate([np.ones((127, 1), np.float32), ov], axis=1)
    em = np.zeros((32, 16, 128), np.float32)
    for j in range(16):
        for k in range(128):
            em[2 * j + k // 64, j, k] = 30000.0
    c["emat"] = em
    return c


CONST_SHAPES = dict(ident=[128, 128], onehot=[32, W], multv=[128, 4, W], amask=[128, 16, 32],
                    bmask=[128, 16, 32], ovl1=[127, 33], emat=[32, 16, 128])

INPUT_SHAPES = dict(x=[SEQ, D_MODEL], norm_w=[DEPTH, D_MODEL], w_in=[DEPTH, D_MODEL, D_IN],
                    w_out=[DEPTH, D_MODEL, D_MODEL], conv_w=[DEPTH, 3, 512], sinks=[DEPTH, 8],
                    cmp_pos=[DEPTH, 2, 32, 64], cmp_w1=[DEPTH, 2, 2048, 128], cmp_w2=[DEPTH, 2, 128, 64],
                    rel_bias=[32, 24], final_norm_w=[D_MODEL])


class Prog:
    def __init__(self, stage="full", dbg=()):
        from contextlib import ExitStack
        self.stage = stage
        self.dbg = set(dbg)
        self.nc = nc = bass.Bass("TRN2", target_bir_lowering=False)
        self.S = Sched(nc)
        self.es = ExitStack()
        self.dr = {}
        for k, shp in INPUT_SHAPES.items():
            self.dr[k] = nc.dram_tensor(k, shp, F32, kind="ExternalInput").ap()
        for k, shp in CONST_SHAPES.items():
            self.dr[k] = nc.dram_tensor("c_" + k, shp, F32, kind="ExternalInput").ap()
        self.out = nc.dram_tensor("out", [SEQ, D_MODEL], F32, kind="ExternalOutput").ap()
        self.Zh = nc.dram_tensor("ztab", [32, 128, W], BF16, kind="Internal")
        self.Z = self.Zh.ap()
        self.xres = nc.dram_tensor("xres", [SEQ, D_MODEL], F32, kind="Internal").ap()
        self.mixd = nc.dram_tensor("mixd", [16, 128, 16, 128], BF16, kind="Internal").ap()
        self.dbg_out = {}
        self.final_keys = []
        self.ps = [self.es.enter_context(nc.psum_tensor("ps%d" % i, [128, 512], F32)) for i in range(8)]
        self._uid = 0

    def sb(self, es, name, shape, dtype):
        self._uid += 1
        return es.enter_context(self.nc.sbuf_tensor("%s_%d" % (name, self._uid), list(shape), dtype))

    def dump(self, name, src_ap, shape, dtype, reads):
        if name not in self.dbg:
            return
        t = self.nc.dram_tensor("dbg_" + name, list(shape), dtype, kind="ExternalOutput").ap()
        self.dbg_out[name] = t
        key = "dbg_" + name
        self.S.dma("sp", t, src_ap, key, reads=reads)
        self.final_keys.append(key)

    def load_consts(self, es):
        S, dr = self.S, self.dr
        self.ident = self.sb(es, "ident", [128, 128], F32)
        S.dma("sp", self.ident[:], dr["ident"], "c_ident", writes=["ident"])
        self.identb = self.sb(es, "identb", [128, 128], BF16)
        S.cp("dve", self.identb[:], self.ident[:], reads=["ident"], writes=["identb"])
        self.eps_t = self.sb(es, "eps_t", [128, 1], F32)
        S.memset("dve", self.eps_t[:], 1e-6, writes=["eps"])

    def load_cols(self, dst, src_vec, n, tmp, psb, key):
        S = self.S
        S.dma("sp", tmp[0:n, :], src_vec.rearrange("(k p) -> k p", p=128), "lc_tmp", writes=["lc_tmp"])
        S.tr(psb[:, 0:n], tmp[0:n, :], self.ident[0:n, 0:n], reads=["lc_tmp", "ident"], writes=["ps7"])
        S.cp("dve", dst, psb[:, 0:n], reads=["ps7"], writes=[key])

    def phase0_tables(self, es_outer=None):
        from contextlib import ExitStack
        S, dr, ps = self.S, self.dr, self.ps
        with ExitStack() as es_inner:
            es = es_outer if es_outer is not None else es_inner
            rb = self.sb(es, "rb", [32, 24], F32)
            rbrep = self.sb(es, "rbrep", [32, 24, 128], F32)
            rbhi = self.sb(es, "rbhi", [128, 24, 128], BF16)
            rblo = self.sb(es, "rblo", [128, 24, 128], BF16)
            oh = self.sb(es, "oh", [128, W], BF16)
            mrep = self.sb(es, "mrep", [128, 4, W], BF16)
            Eh = [self.sb(es, "Eh%d" % i, [128, W], BF16) for i in range(2)]
            gb = [self.sb(es, "gb%d" % i, [128, W], BF16) for i in range(2)]
            S.dma("sp", rb[:], dr["rel_bias"], "t_rb", writes=["rb"])
            S.memset("dve", oh[:], 0.0, writes=["oh"])
            S.dma("pool", oh[0:32, :], dr["onehot"], "t_oh", writes=["oh"])
            S.dma("pool", mrep[:], dr["multv"], "t_mrep", writes=["mrep"])
            rb_b = bass.AP(rb[:].tensor, 0, [[24, 32], [1, 24], [0, 128]])
            S.cp("dve", rbrep[:], rb_b, reads=["rb"], writes=["rbrep"])
            S.memset("pool", rbhi[:], 0.0, writes=["rbhi"])
            S.memset("pool", rblo[:], 0.0, writes=["rblo"])
            S.cp("dve", rbhi[0:32, :, :], rbrep[:], reads=["rbrep"], writes=["rbhi"])
            S.tt("dve", rblo[0:32, :, :], rbrep[:], rbhi[0:32, :, :], ALU.subtract, reads=["rbrep", "rbhi"], writes=["rblo"])
            gi = 0
            pi = 0
            for h in range(24):
                e = h % 2
                for c in range(5):
                    c0 = c * 512
                    n = min(512, W - c0)
                    bank = ps[pi % 2]
                    bk = "ps%d" % (pi % 2)
                    pi += 1
                    S.mm(bank[:, 0:n], rbhi[:, h, :], oh[:, c0:c0 + n], start=True, stop=False,
                         reads=["rbhi", "oh"], writes=[bk])
                    S.mm(bank[:, 0:n], rblo[:, h, :], oh[:, c0:c0 + n], start=False, stop=True,
                         reads=["rblo", "oh"], writes=[bk])
                    S.act(Eh[e][:, c0:c0 + n], bank[:, 0:n], AF.Exp, reads=[bk], writes=[("Eh", e)])
                if h < 8:
                    var = [(0, h), (1, 8 + h)]
                elif h < 16:
                    var = [(2, 8 + h)]
                else:
                    var = [(3, 8 + h)]
                for kind, zi in var:
                    g = gi % 2
                    gi += 1
                    ncol = {0: W, 1: 768, 2: 384, 3: W}[kind]
                    S.tt("dve", gb[g][:, 0:ncol], Eh[e][:, 0:ncol], mrep[:, kind, 0:ncol], ALU.mult,
                         reads=[("Eh", e), "mrep"], writes=[("gb", g)])
                    S.dma("sp", self.Z[zi][:, 0:ncol], gb[g][:, 0:ncol], "zw%d" % g, reads=[("gb", g)], writes=[("Z", zi)])
                    if kind == 0 and getattr(self, "_zc_hook", False):
                        S.dma("sp", self.ZCh.ap()[h][:, 1920:WC], gb[g][:], "zcw%d" % g, reads=[("gb", g)],
                              writes=[("ZC", h)])
            if es_outer is None:
                S.fence()
        self.dump("Z", self.Z, [32, 128, W], BF16, reads=[("Z", i) for i in range(32)])

    def toep(self, zi, x0, n, nrows=128, pstride=W - 1):
        return bass.AP(self.Zh, zi * 128 * W + x0 + 127, [[pstride, nrows], [1, n]])

    def phase1_norm(self, l, es_layer):
        from contextlib import ExitStack
        S, dr, ps = self.S, self.dr, self.ps
        xin = dr["x"] if l == 0 else self.xres
        hnT = self.hnT
        with ExitStack() as es:
            xt = [self.sb(es, "xt%d" % i, [128, 2048], F32) for i in range(2)]
            xn = [self.sb(es, "xn%d" % i, [128, 2048], BF16) for i in range(2)]
            sqj = self.sb(es, "sqj", [128, 2048], BF16)
            ss = self.sb(es, "ss", [128, 16], F32)
            rs = self.sb(es, "rs", [128, 16], F32)
            rstd = self.sb(es, "rstd", [128, 16], F32)
            nw = self.sb(es, "nw", [128, 16], F32)
            tmp = self.sb(es, "lc_tmp", [16, 128], F32)
            self.load_cols(nw[:], dr["norm_w"][l], 16, tmp, ps[7], "nw")
            pi = [0]

            def stage_a(i):
                b = i % 2
                S.dma("act", xt[b][:], xin[i * 128:(i + 1) * 128, :], "xt%d" % b,
                      reads=[("xres", i)], writes=[("xt", b)])
                S.act(sqj[:], xt[b][:], AF.Square, accum_out=ss[:, i:i + 1],
                      reads=[("xt", b)], writes=["sqj", ("ss", i)])
                S.act(rs[:, i:i + 1], ss[:, i:i + 1], AF.Sqrt, scale=1.0 / D_MODEL, bias=self.eps_t[:, 0:1],
                      reads=[("ss", i), "eps"], writes=[("rs", i)])
                S.add("dve", lambda e, i=i: e.reciprocal(rstd[:, i:i + 1], rs[:, i:i + 1]),
                      reads=[("rs", i)], writes=[("rstd", i)])
                S.act(xn[b][:], xt[b][:], AF.Copy, scale=rstd[:, i:i + 1],
                      reads=[("xt", b), ("rstd", i)], writes=[("xn", b)])

            def stage_b(i):
                b = i % 2
                for k4 in range(4):
                    bank = ps[6 + pi[0] % 2]
                    bk = "ps%d" % (6 + pi[0] % 2)
                    pi[0] += 1
                    bankb = bank[:, 0:256].bitcast(BF16)
                    for kk in range(4):
                        k = k4 * 4 + kk
                        S.tr(bankb[:, kk * 128:(kk + 1) * 128], xn[b][:, k * 128:(k + 1) * 128], self.identb[:],
                             reads=[("xn", b), "identb"], writes=[bk])
                    o = hnT[:, k4 * 4:(k4 + 1) * 4, i * 128:(i + 1) * 128]
                    i0 = bankb[:, 0:512].rearrange("p (a b) -> p a b", a=4)
                    nwb = bass.AP(nw[:].tensor, k4 * 4, [[16, 128], [1, 4], [0, 128]])
                    S.tt("dve", o, i0, nwb, ALU.mult, reads=[bk, "nw"], writes=[("hnT", i)])

            stage_a(0)
            for i in range(NT):
                if i + 1 < NT:
                    stage_a(i + 1)
                stage_b(i)
            if hasattr(self, "prefetch_w"):
                self.prefetch_w(l, [(SEG["akc"], 128)])
                self.prefetch_w(l, [(SEG["avc"], 128)])
            S.fence()
            self.dump("ss%d" % l, ss[:], [128, 16], F32, reads=[("ss", i) for i in range(NT)])
            self.dump("rstd%d" % l, rstd[:], [128, 16], F32, reads=[("rstd", i) for i in range(NT)])
            self.dump("xn%d" % l, xn[1][:], [128, 2048], F32, reads=[("xn", 1)])
            self.dump("nw%d" % l, nw[:], [128, 16], F32, reads=["nw"])
        self.dump("hnT%d" % l, hnT[:], [128, 16, 2048], BF16, reads=[("hnT", i) for i in range(NT)])

    def build(self):
        from contextlib import ExitStack
        st = self.stage
        with self.es:
            with ExitStack() as es:
                self.load_consts(es)
                self.phase0_tables()
                if st == "p0":
                    self.S.emit(self.final_keys)
                    return self.nc
                self.hnT = self.sb(es, "hnT", [128, 16, 2048], BF16)
                for l in range(DEPTH):
                    self.phase1_norm(l, es)
                    if st == "p1":
                        break
                    if st == "W":
                        t, wk, M = self.load_w(l, _pair_cols(SEG["bq"], 3))
                        self.dump("wst", t[:], [128, 16, 128], BF16, reads=[wk])
                        break
                self.S.emit(self.final_keys)
        return self.nc


def make_in_maps(inputs, n_cores):
    consts = host_constants()
    maps = []
    for b in range(n_cores):
        m = {}
        for k in INPUT_SHAPES:
            a = np.asarray(inputs[k], dtype=np.float32)
            m[k] = np.ascontiguousarray(a[b]) if k == "x" else np.ascontiguousarray(a)
        for k, v in consts.items():
            m["c_" + k] = np.ascontiguousarray(v, dtype=np.float32)
        maps.append(m)
    return maps


def kernel(**inputs):
    n = 8
    prog = ProgA("full")
    nc = prog.build()
    in_maps = make_in_maps(inputs, n)
    res = run_bass_kernel_spmd(nc, in_maps, core_ids=list(range(n)))
    return np.stack([np.asarray(r["out"], dtype=np.float32) for r in res.results], axis=0)


def _pair_cols(base, p):
    return [(base + 64 * p, 64), (base + 64 * (p + 4), 64)]


class ProgMix(Prog):
    def alloc_common(self, es):
        self.wst = [self.sb(es, "wst%d" % i, [128, 16, 128], BF16) for i in range(3)]
        self.wst_i = 0
        self.pj_i = 0
        self.Pf = [self.sb(es, "Pf%d" % i, [128, 512], BF16) for i in range(4)]
        self.oc = [self.sb(es, "oc%d" % i, [128, 196], F32) for i in range(8)]
        self.oc_i = 0
        self.gate_t = self.sb(es, "gate_t", [128, 2], F32)
        self.Pm = [self.sb(es, "Pm%d" % i, [128, 512], BF16) for i in range(5)]
        self.b_i = 0
        self.o_i = 0
        self.mk_i = 0
        self.pool_share = 0
        self.prefetched = {}

    def prefetch_w(self, l, ranges):
        if l >= DEPTH:
            return
        k = (l, tuple(ranges))
        if k not in self.prefetched:
            self.prefetched[k] = self._load_w(l, ranges)

    def load_w(self, l, ranges):
        k = (l, tuple(ranges))
        if k in self.prefetched:
            return self.prefetched.pop(k)
        return self._load_w(l, ranges)

    def _load_w(self, l, ranges):
        S = self.S
        b = self.wst_i % 3
        self.wst_i += 1
        t = self.wst[b]
        c0 = 0
        wl = self.dr["w_in"][l]
        for (cs, n) in ranges:
            src = wl[:, cs:cs + n].rearrange("(k p) c -> p k c", p=128)
            S.dma("pool", t[:, :, c0:c0 + n], src, "wst%d_%d" % (b, c0), writes=[("wst", b)])
            c0 += n
        return t, ("wst", b), c0

    def proj_fm(self, l, ranges, evac):
        S = self.S
        t, wk, M = self.load_w(l, ranges)
        for s in range(4):
            bi = 6 + self.pj_i % 2
            self.pj_i += 1
            bank, bk = self.ps[bi], "ps%d" % bi
            for k in range(16):
                S.mm(bank[0:M, 0:512], t[:, k, 0:M], self.hnT[:, k, s * 512:(s + 1) * 512],
                     start=(k == 0), stop=(k == 15),
                     reads=[wk] + [("hnT", 4 * s + q) for q in range(4)], writes=[bk])
            evac(s, bank[0:M, 0:512], bk)

    def proj_tm(self, l, ranges, evac):
        S = self.S
        t, wk, M = self.load_w(l, ranges)
        for i4 in range(4):
            bi = 6 + self.pj_i % 2
            self.pj_i += 1
            bank, bk = self.ps[bi], "ps%d" % bi
            for q in range(4):
                i = 4 * i4 + q
                for k in range(16):
                    S.mm(bank[:, q * M:(q + 1) * M], self.hnT[:, k, i * 128:(i + 1) * 128], t[:, k, 0:M],
                         start=(k == 0), stop=(k == 15), reads=[wk, ("hnT", i)], writes=[bk])
            evac(i4, bank[:, 0:4 * M].rearrange("p (a b) -> p a b", a=4), bk)

    LOOKAHEAD = 3

    def attn_begin(self):
        self.pend = []
        self.chains = []

    def chain_add(self, stages):
        self.chains.append(list(stages))

    def chain_step(self):
        for ch in list(self.chains):
            ch.pop(0)()
            if not ch:
                self.chains.remove(ch)

    def attn_flush(self, keep=0):
        while len(self.pend) > keep:
            p = self.pend.pop(0)
            p()
        if keep == 0:
            while self.chains:
                self.chain_step()

    def attn_batch(self, qT, qk, kT, kk, hp, i, jb, Tt, Tk, tcol0, vfn, obank, ok, ocol, first, last,
                   selT=None, selk=None, post=None, nrows=128, qcols=None, kcols=None):
        S = self.S
        n = len(jb)
        bi = self.b_i % 4
        self.b_i += 1
        sbank, sk = self.ps[bi], "ps%d" % bi
        qs = qT[:, i * 128:(i + 1) * 128] if qcols is None else qcols
        nq = 128 if qcols is None else qcols.shape[1]
        wtot = n * nq
        for jj, j in enumerate(jb):
            ks = kT[:, j * 128:(j + 1) * 128] if kcols is None else kcols
            S.mm(sbank[0:nrows, jj * nq:(jj + 1) * nq], ks, qs, start=True, stop=(selT is None),
                 reads=[qk, kk], writes=[sk])
            if selT is not None:
                S.mm(sbank[0:nrows, jj * nq:(jj + 1) * nq], self.emat[:, j, :], selT[:, i * 128:(i + 1) * 128],
                     start=False, stop=True, reads=["emat", selk], writes=[sk])
        self.attn_flush(keep=self.LOOKAHEAD - 1)
        pf, pfk = self.Pf[bi], ("Pf", bi)
        S.act(pf[0:nrows, 0:wtot], sbank[0:nrows, 0:wtot], AF.Exp, scale=0.125, reads=[sk], writes=[pfk])
        mi = self.mk_i % 5
        self.mk_i += 1
        pm, pmk = self.Pm[mi], ("Pm", mi)
        eng = "pool" if (self.pool_share and self.mk_i % self.pool_share == 0) else "dve"
        S.tt(eng, pm[0:nrows, 0:wtot], pf[0:nrows, 0:wtot], Tt[0:nrows, tcol0:tcol0 + wtot], ALU.mult,
             reads=[pfk, Tk], writes=[pmk])
        self.chain_step()

        def pv():
            if qcols is None:
                for jj, j in enumerate(jb):
                    va, vk = vfn(j)
                    S.mm(obank[:, ocol:ocol + 65], pm[0:nrows, jj * 128:(jj + 1) * 128], va,
                         start=(first and jj == 0), stop=(last and jj == n - 1),
                         reads=[pmk, vk], writes=[ok])
            if post is not None:
                post(pm, pmk)
        self.pend.append(pv)

    def evac_copy(self, dst, key, eng="act"):
        S = self.S

        def f(s, bank, bk):
            S.cp(eng, dst[:, s * 512:(s + 1) * 512], bank, reads=[bk], writes=[key])
        return f

    def evac_qz(self, dst, key):
        S = self.S

        def f(s, bank, bk):
            S.cp("act", dst[0:64, 0, s * 512:(s + 1) * 512], bank[0:64, :], reads=[bk], writes=[key])
            S.cp("act", dst[64:128, 1, s * 512:(s + 1) * 512], bank[64:128, :], reads=[bk], writes=[key])
        return f

    def evac_silu(self, dst, key):
        S = self.S

        def f(s, bank, bk):
            S.act(dst[:, s * 512:(s + 1) * 512], bank, AF.Silu, reads=[bk], writes=[key])
        return f

    def finish_chunk(self, Otok, gT, gk, mixc, mc):
        S = self.S
        for i4 in range(4):
            bi = 7
            self.o_i += 1
            bank, bk = self.ps[bi], "ps%d" % bi
            for q in range(4):
                i = 4 * i4 + q
                S.tr(bank[:, q * 128:(q + 1) * 128], Otok[:, i, :], self.ident[:],
                     reads=[("Otok", i), "ident"], writes=[bk])
            S.tt("dve", mixc[:, i4 * 512:(i4 + 1) * 512], bank[:, 0:512], gT[:, i4 * 512:(i4 + 1) * 512], ALU.mult,
                 reads=[bk, gk], writes=["mixc"])
        self.store_chunk(mixc, mc)

    def store_chunk(self, mixc, mc):
        dst = self.mixd[:, :, mc, :].rearrange("i p t -> p i t")
        self.S.dma("sp", dst, mixc[:].rearrange("p (i t) -> p i t", i=16), "mixst", reads=["mixc"],
                   writes=[("mixd", mc)])

    def mixer_B(self, l):
        from contextlib import ExitStack
        S, ps = self.S, self.ps
        with ExitStack() as es:
            KT = self.sb(es, "b_KT", [128, 2048], BF16)
            Va = self.sb(es, "b_Va", [128, 16, 2, 65], BF16)
            esk = self.sb(es, "b_esk", [128, 8], F32)
            QT = [self.sb(es, "b_QT%d" % i, [128, 2, 2048], BF16) for i in range(2)]
            for v in range(2):
                S.memset("pool", QT[v][:], 0.0, writes=[("b_QT", v)])
            gT = self.sb(es, "b_gT", [128, 2048], F32)
            Tb = [self.sb(es, "b_T%d" % i, [128, 256], BF16) for i in range(2)]
            Otok = self.sb(es, "b_Otok", [128, 16, 128], F32)
            mixc = self.sb(es, "b_mixc", [128, 2048], BF16)
            rl = self.sb(es, "b_rl", [128, 16], F32)
            sk_b = bass.AP(self.dr["sinks"].tensor, l * 8, [[0, 128], [1, 8]])
            S.dma("sp", esk[:], sk_b, "b_sk", writes=["esk"])
            S.act(esk[:], esk[:], AF.Exp, reads=["esk"], writes=["esk"])
            self.proj_fm(l, [(SEG["bk"], 128)], self.evac_copy(KT, "b_KT"))
            S.memset("dve", Va[:, :, :, 64:65], 1.0, writes=["b_Va"])

            def ev_v(i4, bank, bk):
                S.cp("dve", Va[:, 4 * i4:4 * i4 + 4, :, 0:64], bank.rearrange("p a (g d) -> p a g d", g=2),
                     reads=[bk], writes=["b_Va"])
            self.proj_tm(l, [(SEG["bv"], 128)], ev_v)
            ti = 0
            for p in range(4):
                qt, qk = QT[p % 2], ("b_QT", p % 2)
                self.proj_fm(l, _pair_cols(SEG["bq"], p), self.evac_qz(qt, qk))
                self.proj_fm(l, _pair_cols(SEG["bgate"], p), self.evac_silu(gT, "b_gT"))
                self.attn_begin()
                for half in range(2):
                    h = p + 4 * half
                    hp = 64 * half
                    tb, tk = Tb[ti % 2], ("b_T", ti % 2)
                    ti += 1
                    S.dma("sp", tb[:], self.toep(16 + h, 0, 256), "b_T%d" % (ti % 2), reads=[("Z", 16 + h)], writes=[tk])
                    for i in range(NT):
                        oi = 4 + self.o_i % 2
                        self.o_i += 1
                        obank, ok = ps[oi], "ps%d" % oi
                        jl = [i, i - 1] if i >= 1 else [0]

                        def post(pm, pmk, i=i, h=h, hp=hp, obank=obank, ok=ok):
                            c = self.oc_i % 8
                            self.oc_i += 1
                            oc, ock = self.oc[c], ("oc", c)
                            S.cp("act", oc[:, 0:65], obank[:, 0:65], reads=[ok], writes=[ock])
                            self.chain_add([
                                lambda: S.ts("dve", rl[:, i:i + 1], oc[:, 64:65], esk[:, h:h + 1], None, ALU.add,
                                             reads=[ock, "esk"], writes=[("b_rl", i)]),
                                lambda: S.add("dve", lambda e: e.reciprocal(rl[:, i:i + 1], rl[:, i:i + 1]),
                                              reads=[("b_rl", i)], writes=[("b_rl", i)]),
                                lambda: S.ts("dve", Otok[:, i, hp:hp + 64], oc[:, 0:64], rl[:, i:i + 1], None, ALU.mult,
                                             reads=[ock, ("b_rl", i)], writes=[("Otok", i)])])
                        self.attn_batch(qt[:, half, :], qk, KT, "b_KT", hp, i, jl, tb, tk, 0,
                                        lambda j, half=half: (Va[:, j, half, :], "b_Va"),
                                        obank, ok, 0, True, True, post=post)
                self.attn_flush()
                self.finish_chunk(Otok, gT, "b_gT", mixc, 4 + p)
            self.prefetch_w(l, [(SEG["cb"], 128)])
            self.prefetch_w(l, [(SEG["cc"], 128)])
            S.fence()

    def build(self):
        from contextlib import ExitStack
        st = self.stage
        with self.es:
            with ExitStack() as es:
                self.load_consts(es)
                self.phase0_tables()
                self.hnT = self.sb(es, "hnT", [128, 16, 2048], BF16)
                self.alloc_common(es)
                for l in range(DEPTH):
                    self.phase1_norm(l, es)
                    if st == "p1":
                        break
                    if st == "W":
                        t, wk, M = self.load_w(l, _pair_cols(SEG["bq"], 3))
                        self.dump("wst", t[:], [128, 16, 128], BF16, reads=[wk])
                        break
                    if st == "B":
                        self.mixer_B(l)
                        break
                self.S.emit(self.final_keys)
        return self.nc


class ProgFull(ProgMix):
    def mixer_D(self, l):
        from contextlib import ExitStack
        S, ps = self.S, self.ps
        with ExitStack() as es:
            QT = [self.sb(es, "d_QT%d" % i, [128, 2, 2048], BF16) for i in range(2)]
            for v in range(2):
                S.memset("pool", QT[v][:], 0.0, writes=[("d_QT", v)])
            KT = [self.sb(es, "d_KT%d" % i, [128, 2048], BF16) for i in range(2)]
            Va = [self.sb(es, "d_Va%d" % i, [128, 16, 2, 65], BF16) for i in range(2)]
            gT = self.sb(es, "d_gT", [128, 2048], F32)
            Td = [self.sb(es, "d_T%d" % i, [128, 2048], BF16) for i in range(2)]
            Otok = self.sb(es, "d_Otok", [128, 16, 128], F32)
            mixc = self.sb(es, "d_mixc", [128, 2048], BF16)
            rl = self.sb(es, "d_rl", [128, 16], F32)
            for v in range(2):
                S.memset("dve", Va[v][:, :, :, 64:65], 1.0, writes=[("d_Va", v)])
            ti = 0
            for p in range(4):
                b = p % 2
                qt, qk = QT[b], ("d_QT", b)
                kt, kk = KT[b], ("d_KT", b)
                va, vk = Va[b], ("d_Va", b)
                self.proj_fm(l, _pair_cols(SEG["dq"], p), self.evac_qz(qt, qk))
                self.proj_fm(l, _pair_cols(SEG["dk"], p), self.evac_copy(kt, kk))
                self.proj_fm(l, _pair_cols(SEG["dgate"], p), self.evac_silu(gT, "d_gT"))

                def ev_v(i4, bank, bk, va=va, vk=vk):
                    S.cp("dve", va[:, 4 * i4:4 * i4 + 4, :, 0:64], bank.rearrange("p a (g d) -> p a g d", g=2),
                         reads=[bk], writes=[vk])
                self.proj_tm(l, _pair_cols(SEG["dv"], p), ev_v)
                if p == 3:
                    self.prefetch_wout(l)
                self.attn_begin()
                for half in range(2):
                    h = p + 4 * half
                    hp = 64 * half
                    td, tk = Td[ti % 2], ("d_T", ti % 2)
                    S.dma("sp", td[:], self.toep(24 + h, 0, 2048), "d_T%d" % (ti % 2), reads=[("Z", 24 + h)], writes=[tk])
                    ti += 1
                    for i in range(NT):
                        oi = 4 + self.o_i % 2
                        self.o_i += 1
                        obank, ok = ps[oi], "ps%d" % oi
                        jl = list(range(i, -1, -1))
                        nb = (len(jl) + 3) // 4
                        for bb in range(nb):
                            jb = jl[4 * bb:4 * bb + 4]
                            post = None
                            if bb == nb - 1:
                                def post(pm, pmk, i=i, hp=hp, obank=obank, ok=ok):
                                    c = self.oc_i % 8
                                    self.oc_i += 1
                                    oc, ock = self.oc[c], ("oc", c)
                                    S.cp("act", oc[:, 0:65], obank[:, 0:65], reads=[ok], writes=[ock])
                                    self.chain_add([
                                        lambda: S.add("dve", lambda e: e.reciprocal(rl[:, i:i + 1], oc[:, 64:65]),
                                                      reads=[ock], writes=[("d_rl", i)]),
                                        lambda: S.ts("dve", Otok[:, i, hp:hp + 64], oc[:, 0:64], rl[:, i:i + 1], None, ALU.mult,
                                                     reads=[ock, ("d_rl", i)], writes=[("Otok", i)])])
                            self.attn_batch(qt[:, half, :], qk, kt, kk, hp, i, jb, td, tk, 128 * (i - jb[0]),
                                            lambda j, half=half, va=va, vk=vk: (va[:, j, half, :], vk),
                                            obank, ok, 0, bb == 0, bb == nb - 1, post=post)
                self.attn_flush()
                self.finish_chunk(Otok, gT, "d_gT", mixc, 12 + p)
            S.fence()

    def mixer_C(self, l):
        from contextlib import ExitStack
        S, ps = self.S, self.ps
        with ExitStack() as es:
            cbT = self.sb(es, "c_cb", [128, 2048], F32)
            ccT = self.sb(es, "c_cc", [128, 2048], F32)
            chT = self.sb(es, "c_ch", [128, 2048], F32)
            gT = self.sb(es, "c_gT", [128, 2048], F32)
            u = self.sb(es, "c_u", [128, 2048], F32)
            y = self.sb(es, "c_y", [128, 2048], F32)
            mixc = self.sb(es, "c_mixc", [128, 2048], BF16)
            cw = self.sb(es, "c_cw", [128, 4, 3], F32)
            tmp = self.sb(es, "c_tmp", [16, 128], F32)
            for c in range(4):
                S.dma("sp", tmp[0:3, :], self.dr["conv_w"][l][:, 128 * c:128 * (c + 1)], "c_tmp", writes=["c_tmp"])
                S.tr(ps[7][:, 0:3], tmp[0:3, :], self.ident[0:3, 0:3], reads=["c_tmp", "ident"], writes=["ps7"])
                S.cp("dve", cw[:, c, :], ps[7][:, 0:3], reads=["ps7"], writes=[("c_cw", c)])
                self.proj_fm(l, [(SEG["cb"] + 128 * c, 128)], self.evac_copy(cbT, "c_cb"))
                self.proj_fm(l, [(SEG["cc"] + 128 * c, 128)], self.evac_copy(ccT, "c_cc"))
                self.proj_fm(l, [(SEG["ch"] + 128 * c, 128)], self.evac_copy(chT, "c_ch"))
                self.proj_fm(l, [(SEG["cgate"] + 128 * c, 128)], self.evac_silu(gT, "c_gT"))
                S.tt("pool", u[:], ccT[:], chT[:], ALU.mult, reads=["c_cc", "c_ch"], writes=["c_u"])
                S.ts("dve", y[:], u[:], cw[:, c, 2:3], None, ALU.mult, reads=["c_u", ("c_cw", c)], writes=["c_y"])
                S.stt(y[:, 1:2048], u[:, 0:2047], cw[:, c, 1:2], y[:, 1:2048], ALU.mult, ALU.add,
                      reads=["c_u", ("c_cw", c), "c_y"], writes=["c_y"])
                S.stt(y[:, 2:2048], u[:, 0:2046], cw[:, c, 0:1], y[:, 2:2048], ALU.mult, ALU.add,
                      reads=["c_u", ("c_cw", c), "c_y"], writes=["c_y"])
                S.tt("dve", y[:], y[:], cbT[:], ALU.mult, reads=["c_y", "c_cb"], writes=["c_y"])
                S.tt("dve", mixc[:], y[:], gT[:], ALU.mult, reads=["c_y", "c_gT"], writes=["mixc"])
                self.store_chunk(mixc, 8 + c)
            self.prefetch_w(l, _pair_cols(SEG["dq"], 0))
            self.prefetch_w(l, _pair_cols(SEG["dk"], 0))
            S.fence()

    def chunk_rows(self, mc):
        g, p = divmod(mc, 4)
        base = 512 * g
        if g == 2:
            return [(base + 128 * p, 128, 0)]
        return [(base + 64 * p, 64, 0), (base + 64 * (p + 4), 64, 64)]

    def prefetch_wout(self, l):
        S = self.S
        S.memset("pool", self.gate_t[:, 0:1], 0.0, writes=[("hnT", i) for i in range(NT)] + ["wo_gate"])
        for mc in range(16):
            for (r0, nr, p0) in self.chunk_rows(mc):
                S.dma("pool", self.hnT[p0:p0 + nr, mc, :], self.dr["w_out"][l][r0:r0 + nr, :], "o_wo%d" % (mc % 4),
                      reads=["wo_gate"], writes=[("o_wo", mc)])

    def phase3_out(self, l):
        from contextlib import ExitStack
        S, ps = self.S, self.ps
        last = (l == DEPTH - 1)
        xin = self.dr["x"] if l == 0 else self.xres
        with ExitStack() as es:
            wo = self.hnT
            mt = [self.sb(es, "o_mt%d" % i, [128, 16, 128], BF16) for i in range(2)]
            xt = [self.sb(es, "o_xt%d" % i, [128, 2048], F32) for i in range(2)]
            xo = [self.sb(es, "o_xo%d" % i, [128, 2048], F32) for i in range(2)]
            if last:
                fnw = self.sb(es, "o_fnw", [128, 2048], F32)
                ss = self.sb(es, "o_ss", [128, 16], F32)
                rs = self.sb(es, "o_rs", [128, 16], F32)
                rstd = self.sb(es, "o_rstd", [128, 16], F32)
                fb = bass.AP(self.dr["final_norm_w"].tensor, 0, [[0, 128], [1, 2048]])
                S.dma("sp", fnw[:], fb, "o_fnw", writes=["o_fnw"])
            def loads(i):
                b = i % 2
                S.dma("sp", mt[b][:], self.mixd[i], "o_mt%d" % b, reads=[("mixd", mc) for mc in range(16)],
                      writes=[("o_mt", b)])
                S.dma("sp", xt[b][:], xin[i * 128:(i + 1) * 128, :], "o_xt%d" % b, reads=[("xres", i)],
                      writes=[("o_xt", b)])

            loads(0)
            for i in range(NT):
                b = i % 2
                if i + 1 < NT:
                    loads(i + 1)
                for s in range(4):
                    bi = 6 + self.pj_i % 2
                    self.pj_i += 1
                    bank, bk = ps[bi], "ps%d" % bi
                    for mc in range(16):
                        S.mm(bank[:, 0:512], mt[b][:, mc, :], wo[:, mc, s * 512:(s + 1) * 512],
                             start=(mc == 0), stop=(mc == 15), reads=[("o_mt", b), ("o_wo", mc)], writes=[bk])
                    S.tt("dve", xo[b][:, s * 512:(s + 1) * 512], bank[:, 0:512], xt[b][:, s * 512:(s + 1) * 512], ALU.add,
                         reads=[bk, ("o_xt", b)], writes=[("o_xo", b)])
                if not last:
                    S.dma("sp", self.xres[i * 128:(i + 1) * 128, :], xo[b][:], "o_st%d" % b, reads=[("o_xo", b)],
                          writes=[("xres", i)])
                else:
                    S.act(xt[b][:], xo[b][:], AF.Square, accum_out=ss[:, i:i + 1],
                          reads=[("o_xo", b)], writes=[("o_xt", b), ("o_ss", i)])
                    S.act(rs[:, i:i + 1], ss[:, i:i + 1], AF.Sqrt, scale=1.0 / D_MODEL, bias=self.eps_t[:, 0:1],
                          reads=[("o_ss", i), "eps"], writes=[("o_rs", i)])
                    S.add("dve", lambda e, i=i: e.reciprocal(rstd[:, i:i + 1], rs[:, i:i + 1]),
                          reads=[("o_rs", i)], writes=[("o_rstd", i)])
                    S.stt(xo[b][:], xo[b][:], rstd[:, i:i + 1], fnw[:], ALU.mult, ALU.mult,
                          reads=[("o_xo", b), ("o_rstd", i), "o_fnw"], writes=[("o_xo", b)])
                    S.dma("sp", self.out[i * 128:(i + 1) * 128, :], xo[b][:], "o_st%d" % b, reads=[("o_xo", b)],
                          writes=[("out", i)])
                    if "o_st%d" % b not in self.final_keys:
                        self.final_keys.append("o_st%d" % b)
            S.fence()
        if not last:
            self.dump("xres", self.xres, [SEQ, D_MODEL], F32, reads=[("xres", i) for i in range(NT)])

    def build(self):
        from contextlib import ExitStack
        st = self.stage
        with self.es:
            with ExitStack() as es:
                self.load_consts(es)
                self.phase0_extra(es)
                self.hnT = self.sb(es, "hnT", [128, 16, 2048], BF16)
                self.alloc_common(es)
                for l in range(DEPTH):
                    if l == 0:
                        with ExitStack() as es0:
                            self.phase0_tables(es0)
                            self.phase1_norm(l, es)
                    else:
                        self.phase1_norm(l, es)
                    if st in ("full", "A") or "A" in st:
                        self.mixer_A(l)
                    if st in ("full",) or "B" in st:
                        self.mixer_B(l)
                    if st in ("full",) or "C" in st:
                        self.mixer_C(l)
                    if st in ("full",) or "D" in st:
                        self.mixer_D(l)
                    if st in ("full",) or "O" in st:
                        self.phase3_out(l)
                    if st != "full":
                        self.dump("mixd", self.mixd, [16, 128, 16, 128], BF16, reads=[("mixd", c) for c in range(16)])
                        break
                self.S.emit(self.final_keys)
        return self.nc


WC = 4224


class ProgA(ProgFull):
    def phase0_extra(self, es):
        S, dr = self.S, self.dr
        self.ZCh = self.nc.dram_tensor("zctab", [8, 128, WC], BF16, kind="Internal")
        self.emat = self.sb(es, "emat", [128, 16, 128], BF16)
        S.memset("pool", self.emat[:], 0.0, writes=["emat"])
        self.ovl1b = self.sb(es, "ovl1b", [128, 33], BF16)
        self.amask = self.sb(es, "amask", [128, 16, 32], F32)
        self.bmask = self.sb(es, "bmask", [128, 16, 32], F32)
        S.dma("pool", self.emat[0:32, :, :], dr["emat"], "c_emat", writes=["emat"])
        S.dma("pool", self.ovl1b[0:127, :], dr["ovl1"], "c_ovl", writes=["ovl1b"])
        S.dma("sp", self.amask[:], dr["amask"], "c_am", writes=["amask"])
        S.dma("sp", self.bmask[:], dr["bmask"], "c_bm", writes=["bmask"])

    def toepC(self, h):
        return bass.AP(self.ZCh, h * 128 * WC + 2016, [[WC - 16, 128], [1, 2048]])

    def phase0_tables(self, es_outer=None):
        from contextlib import ExitStack
        with ExitStack() as es_inner:
            es = es_outer if es_outer is not None else es_inner
            z = self.sb(es, "zc_zero", [128, 1920], BF16)
            self.S.memset("dve", z[:], 0.0, writes=["zc_zero"])
            for h in range(8):
                self.S.dma("sp", self.ZCh.ap()[h][:, 0:1920], z[:], "zc_z", reads=["zc_zero"], writes=[("ZCz", h)])
            self._zc_hook = True
            super().phase0_tables(es_outer)

    def mixer_A(self, l):
        from contextlib import ExitStack
        S, ps, dr = self.S, self.ps, self.dr
        with ExitStack() as es:
            QT = self.sb(es, "a_QT", [128, 4, 2, 2048], BF16)
            S.memset("pool", QT[:], 0.0, writes=[("a_QT", p) for p in range(4)])
            KST = self.sb(es, "a_KST", [128, 2048], BF16)
            KWT = self.sb(es, "a_KWT", [128, 2048], BF16)
            VS = self.sb(es, "a_VS", [128, 16, 2, 65], BF16)
            VW = self.sb(es, "a_VW", [128, 16, 2, 65], BF16)
            gsig = self.sb(es, "a_gsig", [128, 16, 24], F32)
            kcT = self.sb(es, "a_kcT", [128, 128], BF16)
            vca = self.sb(es, "a_vca", [128, 2, 65], BF16)
            impacc = self.sb(es, "a_imp", [128, 16, 2, 32], F32)
            selT = [self.sb(es, "a_selT%d" % g, [128, 2048], BF16) for g in range(2)]
            for g in range(2):
                S.memset("pool", selT[g][:], 0.0, writes=[("a_selT", g)])
            with ExitStack() as es1:
                KCT = self.sb(es1, "a_KCT", [128, 2048], BF16)
                VCT = self.sb(es1, "a_VCT", [128, 2048], BF16)
                w1 = self.sb(es1, "a_w1", [128, 2, 32, 128], BF16)
                w2 = self.sb(es1, "a_w2", [128, 2, 64], BF16)
                ptmp = self.sb(es1, "a_ptmp", [32, 128], F32)
                posT = self.sb(es1, "a_posT", [64, 2, 32], BF16)
                hb = self.sb(es1, "a_hb", [128, 2], F32)
                hs = self.sb(es1, "a_hs", [128, 2, 2, 128], BF16)
                for kind in range(2):
                    for half in range(2):
                        S.dma("pool", w1[64 * half:64 * half + 64, kind, :, :],
                              dr["cmp_w1"][l, kind].rearrange("(l d) n -> d l n", d=64),
                              "a_w1_%d%d" % (kind, half), writes=["a_w1"])
                    S.dma("pool", w2[:, kind, :], dr["cmp_w2"][l, kind], "a_w2_%d" % kind, writes=["a_w2"])
                    S.dma("sp", ptmp[:, kind * 64:(kind + 1) * 64], dr["cmp_pos"][l, kind], "a_pos%d" % kind,
                          writes=["a_ptmp"])
                for kind in range(2):
                    S.tr(ps[7][0:64, 32 * kind:32 * kind + 32], ptmp[:, kind * 64:(kind + 1) * 64], self.ident[0:32, 0:32],
                         reads=["a_ptmp", "ident"], writes=["ps7"])
                    S.cp("dve", posT[:, kind, :], ps[7][0:64, 32 * kind:32 * kind + 32], reads=["ps7"], writes=["a_posT"])
                self.proj_fm(l, [(SEG["akc"], 128)], self.evac_copy(KCT, "a_KCT"))
                self.proj_fm(l, [(SEG["avc"], 128)], self.evac_copy(VCT, "a_VCT"))
                for p in range(4):
                    def ev_q(s, bank, bk, p=p):
                        S.cp("act", QT[0:64, p, 0, s * 512:(s + 1) * 512], bank[0:64, :], reads=[bk], writes=[("a_QT", p)])
                        S.cp("act", QT[64:128, p, 1, s * 512:(s + 1) * 512], bank[64:128, :], reads=[bk], writes=[("a_QT", p)])
                    self.proj_fm(l, _pair_cols(SEG["aq"], p), ev_q)
                for kind, XT, xk in ((0, KCT, "a_KCT"), (1, VCT, "a_VCT")):
                    for l_ in range(32):
                        S.mm(ps[5][:, kind:kind + 1], w1[0:64, kind, l_, :], posT[0:64, kind, l_:l_ + 1],
                             start=(l_ == 0), stop=(l_ == 31), reads=["a_w1", "a_posT"], writes=["ps5"])
                    S.cp("dve", hb[:, kind:kind + 1], ps[5][:, kind:kind + 1], reads=["ps5"], writes=[("a_hb", kind)])
                    for g in range(2):
                        hp = 64 * g
                        bi = 6 + self.pj_i % 2
                        self.pj_i += 1
                        bank, bk = ps[bi], "ps%d" % bi
                        for l_ in range(32):
                            S.mm(bank[:, 0:127], w1[hp:hp + 64, kind, l_, :], XT[hp:hp + 64, l_:l_ + 2017:16],
                                 start=(l_ == 0), stop=(l_ == 31), reads=["a_w1", xk], writes=[bk])
                        S.act(hs[:, kind, g, 0:127], bank[:, 0:127], AF.Silu, bias=hb[:, kind:kind + 1],
                              reads=[bk, ("a_hb", kind)], writes=["a_hs"])
                for g in range(2):
                    S.mm(ps[7][64 * g:64 * g + 64, 0:127], w2[:, 0, :], hs[:, 0, g, 0:127],
                         reads=["a_w2", "a_hs"], writes=["ps7"])
                    S.cp("dve", kcT[64 * g:64 * g + 64, 0:127], ps[7][64 * g:64 * g + 64, 0:127], reads=["ps7"], writes=["a_kcT"])
                S.memset("dve", vca[:, :, 64:65], 1.0, writes=["a_vca"])
                for g in range(2):
                    S.mm(ps[4][0:127, 64 * g:64 * g + 64], hs[:, 1, g, 0:127], w2[:, 1, :],
                         reads=["a_w2", "a_hs"], writes=["ps4"])
                    S.cp("dve", vca[0:127, g, 0:64], ps[4][0:127, 64 * g:64 * g + 64], reads=["ps4"], writes=["a_vca"])
                self.prefetch_w(l, [(SEG["aks"], 128)])
                self.prefetch_w(l, [(SEG["akw"], 128)])
                S.fence()
            self.dump("kcT", kcT[:], [128, 128], BF16, reads=["a_kcT"])
            self.dump("vca", vca[:], [128, 2, 65], BF16, reads=["a_vca"])
            with ExitStack() as es3:
                Tc = [self.sb(es, "a_Tc1_%d" % i, [128, 2048], BF16) for i in range(2)]
                rl4 = [self.sb(es, "a_rl4_%d" % i, [128, 4], F32) for i in range(2)]
                S.memset("pool", impacc[:], 0.0, writes=[("a_imp", i, g) for i in range(NT) for g in range(2)])
                ti = 0
                ri = 0
                self.attn_begin()
                for p in range(4):
                    for half in range(2):
                        h = p + 4 * half
                        hp = 64 * half
                        tc, tck = Tc[ti % 2], ("a_Tc", ti % 2)
                        S.dma("sp", tc[:, :], self.toepC(h), "a_Tc%d" % (ti % 2),
                              reads=[("ZC", h), ("ZCz", h)], writes=[tck])
                        ti += 1
                        for s in range(4):
                            def post(pm, pmk, s=s, half=half):
                                nonlocal ri
                                oi = 4 + self.o_i % 2
                                self.o_i += 1
                                ub, uk = ps[oi], "ps%d" % oi
                                for qq in range(4):
                                    S.mm(ub[:, 33 * qq:33 * qq + 33], pm[0:127, qq * 128:(qq + 1) * 128], self.ovl1b[0:127, :],
                                         reads=[pmk, "ovl1b"], writes=[uk])
                                r4, rk = rl4[ri % 2], ("a_rl4", ri % 2)
                                ri += 1
                                S.ts("dve", r4[:, 0:4], ub[:, 0:132:33], 1e-30, None, ALU.max, reads=[uk], writes=[rk])
                                S.add("dve", lambda e, r4=r4: e.reciprocal(r4[:, 0:4], r4[:, 0:4]), reads=[rk], writes=[rk])
                                for qq in range(4):
                                    i = 4 * s + qq
                                    S.stt(impacc[:, i, half, :], ub[:, 33 * qq + 1:33 * qq + 33], r4[:, qq:qq + 1],
                                          impacc[:, i, half, :], ALU.mult, ALU.add,
                                          reads=[uk, rk, ("a_imp", i, half)], writes=[("a_imp", i, half)])
                            self.attn_batch(None, ("a_QT", p), None, "a_kcT", hp, 0, [0], tc, tck, s * 512,
                                            None, None, None, 0, True, True, post=post, nrows=127,
                                            qcols=QT[:, p, half, s * 512:(s + 1) * 512], kcols=kcT[:, 0:127])
                self.attn_flush()
            self.proj_fm(l, [(SEG["aks"], 128)], self.evac_copy(KST, "a_KST"))
            self.proj_fm(l, [(SEG["akw"], 128)], self.evac_copy(KWT, "a_KWT"))
            for V, vk, seg in ((VS, "a_VS", "avs"), (VW, "a_VW", "avw")):
                S.memset("pool", V[:, :, :, 64:65], 1.0, writes=[vk])

                def ev_v(i4, bank, bk, V=V, vk=vk):
                    S.cp("act", V[:, 4 * i4:4 * i4 + 4, :, 0:64], bank.rearrange("p a (g d) -> p a g d", g=2),
                         reads=[bk], writes=[vk])
                self.proj_tm(l, [(SEG[seg], 128)], ev_v)

            def ev_g(i4, bank, bk):
                S.act(gsig[:, 4 * i4:4 * i4 + 4, :], bank, AF.Sigmoid, reads=[bk], writes=["a_gsig"])
            self.proj_tm(l, [(SEG["agates"], 24)], ev_g)
            self.dump("imp", impacc[:], [128, 16, 2, 32], F32, reads=[("a_imp", i, g) for i in range(NT) for g in range(2)])
            with ExitStack() as es4:
                impf = [self.sb(es, "a_impf%d" % i, [128, 32], F32) for i in range(2)]
                top8 = [self.sb(es, "a_top8_%d" % i, [128, 8], F32) for i in range(2)]
                seln = [self.sb(es, "a_seln%d" % i, [128, 32], BF16) for i in range(2)]
                ci = 0
                for g in range(2):
                    for i4 in range(4):
                        bi = 7
                        self.o_i += 1
                        bank, bk = ps[bi], "ps%d" % bi
                        for q in range(4):
                            i = 4 * i4 + q
                            c = ci % 2
                            ci += 1
                            S.tt("dve", impf[c][:], impacc[:, i, g, :], self.amask[:, i, :], ALU.mult,
                                 reads=[("a_imp", i, g), "amask"], writes=[("a_impf", c)])
                            S.tt("dve", impf[c][:], impf[c][:], self.bmask[:, i, :], ALU.add,
                                 reads=[("a_impf", c), "bmask"], writes=[("a_impf", c)])
                            S.add("dve", lambda e, c=c: e.max(top8[c][:], impf[c][:]),
                                  reads=[("a_impf", c)], writes=[("a_top8", c)])
                            S.ts("dve", seln[c][:], impf[c][:], top8[c][:, 7:8], 1.0, ALU.is_ge, ALU.subtract,
                                 reads=[("a_impf", c), ("a_top8", c)], writes=[("a_seln", c)])
                            S.mm(bank[0:32, q * 128:(q + 1) * 128], seln[c][:], self.identb[:],
                                 reads=[("a_seln", c), "identb"], writes=[bk])
                        S.cp("act", selT[g][0:32, i4 * 512:(i4 + 1) * 512], bank[0:32, 0:512], reads=[bk], writes=[("a_selT", g)])
                self.prefetch_w(l, _pair_cols(SEG["agate"], 0))
            self.dump("selT", selT[0][0:32, :], [32, 2048], BF16, reads=[("a_selT", 0)])
            self.pool_share = 0
            with ExitStack() as es5:
                Tsl = [self.sb(es5, "a_Tsl%d" % i, [128, 2048], BF16) for i in range(2)]
                Twn = [self.sb(es5, "a_Twn%d" % i, [128, 640], BF16) for i in range(2)]
                PmC = [self.sb(es5, "a_PmC%d" % i, [128, 512], BF16) for i in range(2)]
                gT = self.sb(es5, "a_gT", [128, 2048], F32)
                Otok = self.sb(es5, "a_Otok", [128, 16, 128], F32)
                mixc = self.sb(es5, "a_mixc", [128, 2048], BF16)
                rl3 = [self.sb(es5, "a_rl3_%d" % i, [128, 3], F32) for i in range(8)]
                s3 = [self.sb(es5, "a_s3_%d" % i, [128, 3], F32) for i in range(8)]
                ot1 = [self.sb(es5, "a_ot1_%d" % i, [128, 64], F32) for i in range(8)]
                ot2 = [self.sb(es5, "a_ot2_%d" % i, [128, 64], F32) for i in range(8)]
                ti = 0
                ci = 0
                pci = 0
                for p in range(4):
                    self.proj_fm(l, _pair_cols(SEG["agate"], p), self.evac_silu(gT, "a_gT"))
                    self.attn_begin()
                    for half in range(2):
                        h = p + 4 * half
                        hp = 64 * half
                        tb = ti % 2
                        ti += 1
                        tsl, tslk = Tsl[tb], ("a_Tsl", tb)
                        twn, twnk = Twn[tb], ("a_Twn", tb)
                        tc, tck = Tc[tb], ("a_Tc", tb)
                        S.dma("sp", tsl[:], self.toep(h, 0, 2048), "a_Tsl%d" % tb, reads=[("Z", h)], writes=[tslk])
                        S.dma("sp", twn[:], self.toep(8 + h, 0, 640), "a_Twn%d" % tb, reads=[("Z", 8 + h)], writes=[twnk])
                        S.dma("sp", tc[:, :], self.toepC(h), "a_Tc%d" % tb, reads=[("ZC", h), ("ZCz", h)], writes=[tck])
                        pmc = None
                        for i in range(NT):
                            oi = 4 + self.o_i % 2
                            self.o_i += 1
                            obank, ok = ps[oi], "ps%d" % oi
                            if i % 4 == 0:
                                s = i // 4
                                pmc, pmck = PmC[pci % 2], ("a_PmC", pci % 2)
                                pci += 1

                                def postc(pm, pmk, pmc=pmc, pmck=pmck):
                                    S.cp("pool", pmc[0:127, :], pm[0:127, 0:512], reads=[pmk], writes=[pmck])
                                self.attn_batch(None, ("a_QT", p), None, "a_kcT", hp, 0, [0], tc, tck, s * 512,
                                                None, None, None, 0, True, True, post=postc, nrows=127,
                                                qcols=QT[:, p, half, s * 512:(s + 1) * 512],
                                                kcols=kcT[:, 0:127])
                            jl = list(range(i, -1, -1))
                            nb = (len(jl) + 3) // 4
                            for bb in range(nb):
                                jb = jl[4 * bb:4 * bb + 4]
                                self.attn_batch(QT[:, p, half, :], ("a_QT", p), KST, "a_KST", hp, i, jb, tsl, tslk,
                                                128 * (i - jb[0]),
                                                lambda j, half=half: (VS[:, j, half, :], "a_VS"),
                                                obank, ok, 65, bb == 0, bb == nb - 1,
                                                selT=selT[half], selk=("a_selT", half))
                            jl = list(range(i, max(0, i - 4) - 1, -1))
                            nb = (len(jl) + 3) // 4
                            for bb in range(nb):
                                jb = jl[4 * bb:4 * bb + 4]
                                post = None
                                if bb == nb - 1:
                                    c = ci % 8
                                    ci += 1

                                    def post(pm, pmk, i=i, h=h, hp=hp, half=half, obank=obank, ok=ok, c=c, pmc=pmc, pmck=pmck):
                                        S.mm(obank[:, 0:65], pmc[0:127, (i % 4) * 128:(i % 4 + 1) * 128], vca[0:127, half, :],
                                             reads=[pmck, "a_vca"], writes=[ok])
                                        cc = self.oc_i % 8
                                        self.oc_i += 1
                                        oc, ock = self.oc[cc], ("oc", cc)
                                        S.cp("act", oc[:, 0:195], obank[:, 0:195], reads=[ok], writes=[ock])
                                        self.chain_add([
                                            lambda: S.ts("dve", rl3[c][:], oc[:, 64:195:65], 1e-30, None, ALU.max,
                                                         reads=[ock], writes=[("a_rl3", c)]),
                                            lambda: S.add("dve", lambda e: e.reciprocal(rl3[c][:], rl3[c][:]),
                                                          reads=[("a_rl3", c)], writes=[("a_rl3", c)]),
                                            lambda: S.tt("dve", s3[c][:], rl3[c][:], gsig[:, i, h:24:8], ALU.mult,
                                                         reads=[("a_rl3", c), "a_gsig"], writes=[("a_s3", c)]),
                                            lambda: S.ts("dve", ot1[c][:], oc[:, 0:64], s3[c][:, 0:1], None, ALU.mult,
                                                         reads=[ock, ("a_s3", c)], writes=[("a_ot1", c)]),
                                            lambda: S.stt(ot2[c][:], oc[:, 65:129], s3[c][:, 1:2], ot1[c][:], ALU.mult, ALU.add,
                                                          reads=[ock, ("a_s3", c), ("a_ot1", c)], writes=[("a_ot2", c)]),
                                            lambda: S.stt(Otok[:, i, hp:hp + 64], oc[:, 130:194], s3[c][:, 2:3], ot2[c][:],
                                                          ALU.mult, ALU.add,
                                                          reads=[ock, ("a_s3", c), ("a_ot2", c)], writes=[("Otok", i)])])
                                self.attn_batch(QT[:, p, half, :], ("a_QT", p), KWT, "a_KWT", hp, i, jb, twn, twnk,
                                                128 * (i - jb[0]),
                                                lambda j, half=half: (VW[:, j, half, :], "a_VW"),
                                                obank, ok, 130, bb == 0, bb == nb - 1, post=post)
                    self.attn_flush()
                    self.finish_chunk(Otok, gT, "a_gT", mixc, p)
                S.fence()
            self.pool_share = 0
            self.prefetch_w(l, [(SEG["bk"], 128)])
            self.prefetch_w(l, [(SEG["bv"], 128)])
            S.fence()
```

```python
import numpy as np
import concourse.bass as bass
import concourse.mybir as mybir
from concourse.bass_utils import run_bass_kernel_spmd

F32 = mybir.dt.float32
BF16 = mybir.dt.bfloat16
AF = mybir.ActivationFunctionType
ALU = mybir.AluOpType

D_MODEL = 2048
SEQ = 2048
DEPTH = 2
D_IN = 7192
NT = 16
W = 2304
SAME_ENGINE_SYNC = True

SEG = dict(aq=0, akc=512, avc=640, aks=768, avs=896, akw=1024, avw=1152, agates=1280, agate=1304,
           bq=1816, bk=2328, bv=2456, bgate=2584,
           cb=3096, cc=3608, ch=4120, cgate=4632,
           dq=5144, dk=5656, dv=6168, dgate=6680)


class _Op:
    __slots__ = ("eng", "fn", "deps", "is_dma", "dkey", "dval", "sig", "sval", "idx")


class Sched:
    ENGS = ("pe", "act", "dve", "pool", "sp")

    def __init__(self, nc):
        self.nc = nc
        self.ops = []
        self.last_w = {}
        self.readers = {}
        self.dma_cnt = {}
        self.dma_last = {}
        self.last_eng = {}
        self.fence_deps = []
        self.fence_pending = set()

    def fence(self):
        self.fence_deps = sorted(set(list(self.last_eng.values()) + list(self.dma_last.values())))
        self.fence_pending = set(self.ENGS)

    def add(self, eng, fn, reads=(), writes=(), dkey=None):
        op = _Op()
        op.eng = eng
        op.fn = fn
        op.is_dma = dkey is not None
        op.dkey = dkey
        op.sig = False
        op.sval = 0
        op.dval = 0
        op.idx = len(self.ops)
        deps = {}
        for k in reads:
            w = self.last_w.get(k)
            if w is not None:
                deps[w] = True
        for k in writes:
            w = self.last_w.get(k)
            if w is not None:
                deps.setdefault(w, False)
            for r in self.readers.get(k, ()):
                deps.setdefault(r, False)
        if dkey is not None:
            prev = self.dma_last.get(dkey)
            if prev is not None:
                deps.setdefault(prev, False)
            self.dma_cnt[dkey] = self.dma_cnt.get(dkey, 0) + 1
            op.dval = 16 * self.dma_cnt[dkey]
            self.dma_last[dkey] = op.idx
        if eng in self.fence_pending:
            self.fence_pending.discard(eng)
            for d in self.fence_deps:
                deps.setdefault(d, False)
        deps.pop(op.idx, None)
        op.deps = sorted(deps.items())
        if dkey is None:
            self.last_eng[eng] = op.idx
        for k in reads:
            self.readers.setdefault(k, []).append(op.idx)
        for k in writes:
            self.last_w[k] = op.idx
            self.readers[k] = []
        self.ops.append(op)
        return op

    def mm(self, out, lhsT, rhs, start=True, stop=True, reads=(), writes=()):
        return self.add("pe", lambda e: e.matmul(out, lhsT, rhs, start=start, stop=stop), reads, writes)

    def tr(self, out, in_, ident, reads=(), writes=()):
        return self.add("pe", lambda e: e.transpose(out, in_, ident), reads, writes)

    def act(self, out, in_, func, reads=(), writes=(), eng="act", **kw):
        return self.add(eng, lambda e: e.activation(out, in_, func, **kw), reads, writes)

    def tt(self, eng, out, in0, in1, op, reads=(), writes=()):
        return self.add(eng, lambda e: e.tensor_tensor(out, in0, in1, op), reads, writes)

    def ts(self, eng, out, in0, s1, s2, op0, op1=None, reads=(), writes=()):
        if op1 is None:
            return self.add(eng, lambda e: e.tensor_scalar(out, in0, s1, None, op0), reads, writes)
        return self.add(eng, lambda e: e.tensor_scalar(out, in0, s1, s2, op0, op1), reads, writes)

    def stt(self, out, in0, scalar, in1, op0, op1, reads=(), writes=()):
        return self.add("dve", lambda e: e.scalar_tensor_tensor(out, in0, scalar, in1, op0, op1), reads, writes)

    def cp(self, eng, out, in_, reads=(), writes=()):
        if eng == "act":
            return self.add(eng, lambda e: e.copy(out, in_), reads, writes)
        return self.add(eng, lambda e: e.tensor_copy(out, in_), reads, writes)

    def memset(self, eng, ap, val, writes=()):
        return self.add(eng, lambda e: e.memset(ap, val), (), writes)

    def dma(self, eng, out, in_, dkey, reads=(), writes=(), **kw):
        return self.add(eng, lambda e: e.dma_start(out, in_, **kw), reads, writes, dkey=dkey)

    def emit(self, final_waits=()):
        nc = self.nc
        ops = self.ops
        def same_ok(y, op, raw):
            return y.eng == op.eng and (y.eng in ("pe", "sp") or not raw or not SAME_ENGINE_SYNC)

        for op in ops:
            for d, raw in op.deps:
                y = ops[d]
                if y.is_dma:
                    continue
                if not same_ok(y, op, raw):
                    y.sig = True
        cnt = {e: 0 for e in self.ENGS}
        for op in ops:
            if op.sig and not op.is_dma:
                cnt[op.eng] += 1
                op.sval = cnt[op.eng]
        esem = {e: nc.alloc_semaphore("s_" + e) for e in self.ENGS}
        dsem = {k: nc.alloc_semaphore("d_%s" % (k,)) for k in self.dma_cnt}
        per_eng = {e: [op for op in ops if op.eng == e] for e in self.ENGS}
        all_sems = list(esem.values()) + list(dsem.values())
        for sm in all_sems:
            nc.gpsimd.sem_clear(sm)
        nc.all_engine_barrier()

        def run(engname, eng):
            waited = {}
            for op in per_eng[engname]:
                need = {}
                for d, raw in op.deps:
                    y = ops[d]
                    if y.is_dma:
                        s, v = ("d", y.dkey), y.dval
                    elif same_ok(y, op, raw):
                        continue
                    else:
                        s, v = ("e", y.eng), y.sval
                    if need.get(s, 0) < v:
                        need[s] = v
                for s, v in need.items():
                    if waited.get(s, 0) >= v:
                        continue
                    waited[s] = v
                    eng.wait_ge(dsem[s[1]] if s[0] == "d" else esem[s[1]], v)
                ins = op.fn(eng)
                if op.is_dma:
                    ins.then_inc(dsem[op.dkey], 16)
                elif op.sig:
                    ins.then_inc(esem[op.eng], 1)
            if engname == "sp":
                for k in final_waits:
                    eng.wait_ge(dsem[k], 16 * self.dma_cnt[k])

        with nc.Block() as block:
            @block.tensor
            def _(e):
                run("pe", e)

            @block.scalar
            def _(e):
                run("act", e)

            @block.vector
            def _(e):
                run("dve", e)

            @block.gpsimd
            def _(e):
                run("pool", e)

            @block.sync
            def _(e):
                run("sp", e)

        nc.all_engine_barrier()
        for sm in all_sems:
            nc.gpsimd.sem_clear(sm)
        nc.all_engine_barrier()


def _t5_bucket_np(d):
    d = np.maximum(d, 0)
    dl = np.maximum(d, 16).astype(np.float32)
    large = 16 + (np.log(dl / np.float32(16)) / np.float32(np.log(2048 / 16)) * np.float32(16)).astype(np.int32)
    return np.where(d < 16, d, np.minimum(large, 31))


def host_constants():
    c = {}
    c["ident"] = np.eye(128, dtype=np.float32)
    idx = np.arange(W)
    d = idx - 127
    bk = _t5_bucket_np(d)
    oh = np.zeros((32, W), np.float32)
    oh[bk, idx] = 1.0
    oh[:, d < 0] = 0.0
    c["onehot"] = oh
    mv = np.zeros((4, W), np.float32)
    nn = d >= 0
    mv[0] = nn
    mv[1] = nn & (d <= 511)
    mv[2] = nn & (d <= 127)
    mv[3] = (nn & (d <= 128)).astype(np.float32) + (nn & (d % 4 == 0) & (d <= 512)) + (nn & (d % 16 == 0) & (d <= 2047))
    c["multv"] = np.ascontiguousarray(np.broadcast_to(mv[None], (128, 4, W))).astype(np.float32)
    t = np.arange(SEQ)
    cur = (t // 64)[:, None]
    blk = np.arange(32)[None]
    forced = (blk == 0) | (blk == cur) | (blk == cur - 1)
    am = np.where((blk > cur) | forced, 0.0, 1.0).astype(np.float32)
    bm = np.where(blk > cur, -1e30, np.where(forced, 1e4, 0.0)).astype(np.float32)
    c["amask"] = am.reshape(16, 128, 32).transpose(1, 0, 2).copy()
    c["bmask"] = bm.reshape(16, 128, 32).transpose(1, 0, 2).copy()
    cs = np.arange(127) * 16
    ss = np.arange(32) * 64
    ov = ((cs[:, None] < ss[None] + 64) & (cs[:, None] + 32 > ss[None])).astype(np.float32)
    c["ovl1"] = np.concatenate([np.ones((127, 1), np.float32), ov], axis=1)
    em = np.zeros((32, 16, 128), np.float32)
    for j in range(16):
        for k in range(128):
            em[2 * j + k // 64, j, k] = 30000.0
    c["emat"] = em
    return c


CONST_SHAPES = dict(ident=[128, 128], onehot=[32, W], multv=[128, 4, W], amask=[128, 16, 32],
                    bmask=[128, 16, 32], ovl1=[127, 33], emat=[32, 16, 128])

INPUT_SHAPES = dict(x=[SEQ, D_MODEL], norm_w=[DEPTH, D_MODEL], w_in=[DEPTH, D_MODEL, D_IN],
                    w_out=[DEPTH, D_MODEL, D_MODEL], conv_w=[DEPTH, 3, 512], sinks=[DEPTH, 8],
                    cmp_pos=[DEPTH, 2, 32, 64], cmp_w1=[DEPTH, 2, 2048, 128], cmp_w2=[DEPTH, 2, 128, 64],
                    rel_bias=[32, 24], final_norm_w=[D_MODEL])


class Prog:
    def __init__(self, stage="full", dbg=()):
        from contextlib import ExitStack
        self.stage = stage
        self.dbg = set(dbg)
        self.nc = nc = bass.Bass("TRN2", target_bir_lowering=False)
        self.S = Sched(nc)
        self.es = ExitStack()
        self.dr = {}
        for k, shp in INPUT_SHAPES.items():
            self.dr[k] = nc.dram_tensor(k, shp, F32, kind="ExternalInput").ap()
        for k, shp in CONST_SHAPES.items():
            self.dr[k] = nc.dram_tensor("c_" + k, shp, F32, kind="ExternalInput").ap()
        self.out = nc.dram_tensor("out", [SEQ, D_MODEL], F32, kind="ExternalOutput").ap()
        self.Zh = nc.dram_tensor("ztab", [32, 128, W], BF16, kind="Internal")
        self.Z = self.Zh.ap()
        self.xres = nc.dram_tensor("xres", [SEQ, D_MODEL], F32, kind="Internal").ap()
        self.mixd = nc.dram_tensor("mixd", [16, 128, 16, 128], BF16, kind="Internal").ap()
        self.dbg_out = {}
        self.final_keys = []
        self.ps = [self.es.enter_context(nc.psum_tensor("ps%d" % i, [128, 512], F32)) for i in range(8)]
        self._uid = 0

    def sb(self, es, name, shape, dtype):
        self._uid += 1
        return es.enter_context(self.nc.sbuf_tensor("%s_%d" % (name, self._uid), list(shape), dtype))

    def dump(self, name, src_ap, shape, dtype, reads):
        if name not in self.dbg:
            return
        t = self.nc.dram_tensor("dbg_" + name, list(shape), dtype, kind="ExternalOutput").ap()
        self.dbg_out[name] = t
        key = "dbg_" + name
        self.S.dma("sp", t, src_ap, key, reads=reads)
        self.final_keys.append(key)

    def load_consts(self, es):
        S, dr = self.S, self.dr
        self.ident = self.sb(es, "ident", [128, 128], F32)
        S.dma("sp", self.ident[:], dr["ident"], "c_ident", writes=["ident"])
        self.identb = self.sb(es, "identb", [128, 128], BF16)
        S.cp("dve", self.identb[:], self.ident[:], reads=["ident"], writes=["identb"])
        self.eps_t = self.sb(es, "eps_t", [128, 1], F32)
        S.memset("dve", self.eps_t[:], 1e-6, writes=["eps"])

    def load_cols(self, dst, src_vec, n, tmp, psb, key):
        S = self.S
        S.dma("sp", tmp[0:n, :], src_vec.rearrange("(k p) -> k p", p=128), "lc_tmp", writes=["lc_tmp"])
        S.tr(psb[:, 0:n], tmp[0:n, :], self.ident[0:n, 0:n], reads=["lc_tmp", "ident"], writes=["ps7"])
        S.cp("dve", dst, psb[:, 0:n], reads=["ps7"], writes=[key])

    def phase0_tables(self, es_outer=None):
        from contextlib import ExitStack
        S, dr, ps = self.S, self.dr, self.ps
        with ExitStack() as es_inner:
            es = es_outer if es_outer is not None else es_inner
            rb = self.sb(es, "rb", [32, 24], F32)
            rbrep = self.sb(es, "rbrep", [32, 24, 128], F32)
            rbhi = self.sb(es, "rbhi", [128, 24, 128], BF16)
            rblo = self.sb(es, "rblo", [128, 24, 128], BF16)
            oh = self.sb(es, "oh", [128, W], BF16)
            mrep = self.sb(es, "mrep", [128, 4, W], BF16)
            Eh = [self.sb(es, "Eh%d" % i, [128, W], BF16) for i in range(2)]
            gb = [self.sb(es, "gb%d" % i, [128, W], BF16) for i in range(2)]
            S.dma("sp", rb[:], dr["rel_bias"], "t_rb", writes=["rb"])
            S.memset("dve", oh[:], 0.0, writes=["oh"])
            S.dma("pool", oh[0:32, :], dr["onehot"], "t_oh", writes=["oh"])
            S.dma("pool", mrep[:], dr["multv"], "t_mrep", writes=["mrep"])
            rb_b = bass.AP(rb[:].tensor, 0, [[24, 32], [1, 24], [0, 128]])
            S.cp("dve", rbrep[:], rb_b, reads=["rb"], writes=["rbrep"])
            S.memset("pool", rbhi[:], 0.0, writes=["rbhi"])
            S.memset("pool", rblo[:], 0.0, writes=["rblo"])
            S.cp("dve", rbhi[0:32, :, :], rbrep[:], reads=["rbrep"], writes=["rbhi"])
            S.tt("dve", rblo[0:32, :, :], rbrep[:], rbhi[0:32, :, :], ALU.subtract, reads=["rbrep", "rbhi"], writes=["rblo"])
            gi = 0
            pi = 0
            for h in range(24):
                e = h % 2
                for c in range(5):
                    c0 = c * 512
                    n = min(512, W - c0)
                    bank = ps[pi % 2]
                    bk = "ps%d" % (pi % 2)
                    pi += 1
                    S.mm(bank[:, 0:n], rbhi[:, h, :], oh[:, c0:c0 + n], start=True, stop=False,
                         reads=["rbhi", "oh"], writes=[bk])
                    S.mm(bank[:, 0:n], rblo[:, h, :], oh[:, c0:c0 + n], start=False, stop=True,
                         reads=["rblo", "oh"], writes=[bk])
                    S.act(Eh[e][:, c0:c0 + n], bank[:, 0:n], AF.Exp, reads=[bk], writes=[("Eh", e)])
                if h < 8:
                    var = [(0, h), (1, 8 + h)]
                elif h < 16:
                    var = [(2, 8 + h)]
                else:
                    var = [(3, 8 + h)]
                for kind, zi in var:
                    g = gi % 2
                    gi += 1
                    ncol = {0: W, 1: 768, 2: 384, 3: W}[kind]
                    S.tt("dve", gb[g][:, 0:ncol], Eh[e][:, 0:ncol], mrep[:, kind, 0:ncol], ALU.mult,
                         reads=[("Eh", e), "mrep"], writes=[("gb", g)])
                    S.dma("sp", self.Z[zi][:, 0:ncol], gb[g][:, 0:ncol], "zw%d" % g, reads=[("gb", g)], writes=[("Z", zi)])
                    if kind == 0 and getattr(self, "_zc_hook", False):
                        S.dma("sp", self.ZCh.ap()[h][:, 1920:WC], gb[g][:], "zcw%d" % g, reads=[("gb", g)],
                              writes=[("ZC", h)])
            if es_outer is None:
                S.fence()
        self.dump("Z", self.Z, [32, 128, W], BF16, reads=[("Z", i) for i in range(32)])

    def toep(self, zi, x0, n, nrows=128, pstride=W - 1):
        return bass.AP(self.Zh, zi * 128 * W + x0 + 127, [[pstride, nrows], [1, n]])

    def phase1_norm(self, l, es_layer):
        from contextlib import ExitStack
        S, dr, ps = self.S, self.dr, self.ps
        xin = dr["x"] if l == 0 else self.xres
        hnT = self.hnT
        with ExitStack() as es:
            xt = [self.sb(es, "xt%d" % i, [128, 2048], F32) for i in range(2)]
            xn = [self.sb(es, "xn%d" % i, [128, 2048], BF16) for i in range(2)]
            sqj = self.sb(es, "sqj", [128, 2048], BF16)
            ss = self.sb(es, "ss", [128, 16], F32)
            rs = self.sb(es, "rs", [128, 16], F32)
            rstd = self.sb(es, "rstd", [128, 16], F32)
            nw = self.sb(es, "nw", [128, 16], F32)
            tmp = self.sb(es, "lc_tmp", [16, 128], F32)
            self.load_cols(nw[:], dr["norm_w"][l], 16, tmp, ps[7], "nw")
            pi = [0]

            def stage_a(i):
                b = i % 2
                S.dma("sp", xt[b][:], xin[i * 128:(i + 1) * 128, :], "xt%d" % b,
                      reads=[("xres", i)], writes=[("xt", b)])
                S.act(sqj[:], xt[b][:], AF.Square, accum_out=ss[:, i:i + 1],
                      reads=[("xt", b)], writes=["sqj", ("ss", i)])
                S.act(rs[:, i:i + 1], ss[:, i:i + 1], AF.Sqrt, scale=1.0 / D_MODEL, bias=self.eps_t[:, 0:1],
                      reads=[("ss", i), "eps"], writes=[("rs", i)])
                S.add("dve", lambda e, i=i: e.reciprocal(rstd[:, i:i + 1], rs[:, i:i + 1]),
                      reads=[("rs", i)], writes=[("rstd", i)])
                S.act(xn[b][:], xt[b][:], AF.Copy, scale=rstd[:, i:i + 1],
                      reads=[("xt", b), ("rstd", i)], writes=[("xn", b)])

            def stage_b(i):
                b = i % 2
                for k4 in range(4):
                    bank = ps[6 + pi[0] % 2]
                    bk = "ps%d" % (6 + pi[0] % 2)
                    pi[0] += 1
                    bankb = bank[:, 0:256].bitcast(BF16)
                    for kk in range(4):
                        k = k4 * 4 + kk
                        S.tr(bankb[:, kk * 128:(kk + 1) * 128], xn[b][:, k * 128:(k + 1) * 128], self.identb[:],
                             reads=[("xn", b), "identb"], writes=[bk])
                    o = hnT[:, k4 * 4:(k4 + 1) * 4, i * 128:(i + 1) * 128]
                    i0 = bankb[:, 0:512].rearrange("p (a b) -> p a b", a=4)
                    nwb = bass.AP(nw[:].tensor, k4 * 4, [[16, 128], [1, 4], [0, 128]])
                    S.tt("dve", o, i0, nwb, ALU.mult, reads=[bk, "nw"], writes=[("hnT", i)])

            stage_a(0)
            for i in range(NT):
                if i + 1 < NT:
                    stage_a(i + 1)
                stage_b(i)
            if hasattr(self, "prefetch_w"):
                self.prefetch_w(l, [(SEG["akc"], 128)])
                self.prefetch_w(l, [(SEG["avc"], 128)])
            S.fence()
            self.dump("ss%d" % l, ss[:], [128, 16], F32, reads=[("ss", i) for i in range(NT)])
            self.dump("rstd%d" % l, rstd[:], [128, 16], F32, reads=[("rstd", i) for i in range(NT)])
            self.dump("xn%d" % l, xn[1][:], [128, 2048], F32, reads=[("xn", 1)])
            self.dump("nw%d" % l, nw[:], [128, 16], F32, reads=["nw"])
        self.dump("hnT%d" % l, hnT[:], [128, 16, 2048], BF16, reads=[("hnT", i) for i in range(NT)])

    def build(self):
        from contextlib import ExitStack
        st = self.stage
        with self.es:
            with ExitStack() as es:
                self.load_consts(es)
                self.phase0_tables()
                if st == "p0":
                    self.S.emit(self.final_keys)
                    return self.nc
                self.hnT = self.sb(es, "hnT", [128, 16, 2048], BF16)
                for l in range(DEPTH):
                    self.phase1_norm(l, es)
                    if st == "p1":
                        break
                    if st == "W":
                        t, wk, M = self.load_w(l, _pair_cols(SEG["bq"], 3))
                        self.dump("wst", t[:], [128, 16, 128], BF16, reads=[wk])
                        break
                self.S.emit(self.final_keys)
        return self.nc


def make_in_maps(inputs, n_cores):
    consts = host_constants()
    maps = []
    for b in range(n_cores):
        m = {}
        for k in INPUT_SHAPES:
            a = np.asarray(inputs[k], dtype=np.float32)
            m[k] = np.ascontiguousarray(a[b]) if k == "x" else np.ascontiguousarray(a)
        for k, v in consts.items():
            m["c_" + k] = np.ascontiguousarray(v, dtype=np.float32)
        maps.append(m)
    return maps


def kernel(**inputs):
    n = 8
    prog = ProgA("full")
    nc = prog.build()
    in_maps = make_in_maps(inputs, n)
    res = run_bass_kernel_spmd(nc, in_maps, core_ids=list(range(n)))
    return np.stack([np.asarray(r["out"], dtype=np.float32) for r in res.results], axis=0)


def _pair_cols(base, p):
    return [(base + 64 * p, 64), (base + 64 * (p + 4), 64)]


class ProgMix(Prog):
    def alloc_common(self, es):
        self.wst = [self.sb(es, "wst%d" % i, [128, 16, 128], BF16) for i in range(3)]
        self.wst_i = 0
        self.pj_i = 0
        self.Pf = [self.sb(es, "Pf%d" % i, [128, 512], BF16) for i in range(4)]
        self.oc = [self.sb(es, "oc%d" % i, [128, 196], F32) for i in range(8)]
        self.oc_i = 0
        self.gate_t = self.sb(es, "gate_t", [128, 2], F32)
        self.Pm = [self.sb(es, "Pm%d" % i, [128, 512], BF16) for i in range(5)]
        self.b_i = 0
        self.o_i = 0
        self.mk_i = 0
        self.pool_share = 0
        self.prefetched = {}

    def prefetch_w(self, l, ranges):
        if l >= DEPTH:
            return
        k = (l, tuple(ranges))
        if k not in self.prefetched:
            self.prefetched[k] = self._load_w(l, ranges)

    def load_w(self, l, ranges):
        k = (l, tuple(ranges))
        if k in self.prefetched:
            return self.prefetched.pop(k)
        return self._load_w(l, ranges)

    def _load_w(self, l, ranges):
        S = self.S
        b = self.wst_i % 3
        self.wst_i += 1
        t = self.wst[b]
        c0 = 0
        wl = self.dr["w_in"][l]
        for (cs, n) in ranges:
            src = wl[:, cs:cs + n].rearrange("(k p) c -> p k c", p=128)
            S.dma("pool", t[:, :, c0:c0 + n], src, "wst%d_%d" % (b, c0), writes=[("wst", b)])
            c0 += n
        return t, ("wst", b), c0

    def proj_fm(self, l, ranges, evac):
        S = self.S
        t, wk, M = self.load_w(l, ranges)
        for s in range(4):
            bi = 6 + self.pj_i % 2
            self.pj_i += 1
            bank, bk = self.ps[bi], "ps%d" % bi
            for k in range(16):
                S.mm(bank[0:M, 0:512], t[:, k, 0:M], self.hnT[:, k, s * 512:(s + 1) * 512],
                     start=(k == 0), stop=(k == 15),
                     reads=[wk] + [("hnT", 4 * s + q) for q in range(4)], writes=[bk])
            evac(s, bank[0:M, 0:512], bk)

    def proj_tm(self, l, ranges, evac):
        S = self.S
        t, wk, M = self.load_w(l, ranges)
        for i4 in range(4):
            bi = 6 + self.pj_i % 2
            self.pj_i += 1
            bank, bk = self.ps[bi], "ps%d" % bi
            for q in range(4):
                i = 4 * i4 + q
                for k in range(16):
                    S.mm(bank[:, q * M:(q + 1) * M], self.hnT[:, k, i * 128:(i + 1) * 128], t[:, k, 0:M],
                         start=(k == 0), stop=(k == 15), reads=[wk, ("hnT", i)], writes=[bk])
            evac(i4, bank[:, 0:4 * M].rearrange("p (a b) -> p a b", a=4), bk)

    LOOKAHEAD = 3

    def attn_begin(self):
        self.pend = []
        self.chains = []

    def chain_add(self, stages):
        self.chains.append(list(stages))

    def chain_step(self):
        for ch in list(self.chains):
            ch.pop(0)()
            if not ch:
                self.chains.remove(ch)

    def attn_flush(self, keep=0):
        while len(self.pend) > keep:
            p = self.pend.pop(0)
            p()
        if keep == 0:
            while self.chains:
                self.chain_step()

    def attn_batch(self, qT, qk, kT, kk, hp, i, jb, Tt, Tk, tcol0, vfn, obank, ok, ocol, first, last,
                   selT=None, selk=None, post=None, nrows=128, qcols=None, kcols=None):
        S = self.S
        n = len(jb)
        bi = self.b_i % 4
        self.b_i += 1
        sbank, sk = self.ps[bi], "ps%d" % bi
        qs = qT[:, i * 128:(i + 1) * 128] if qcols is None else qcols
        nq = 128 if qcols is None else qcols.shape[1]
        wtot = n * nq
        for jj, j in enumerate(jb):
            ks = kT[:, j * 128:(j + 1) * 128] if kcols is None else kcols
            S.mm(sbank[0:nrows, jj * nq:(jj + 1) * nq], ks, qs, start=True, stop=(selT is None),
                 reads=[qk, kk], writes=[sk])
            if selT is not None:
                S.mm(sbank[0:nrows, jj * nq:(jj + 1) * nq], self.emat[:, j, :], selT[:, i * 128:(i + 1) * 128],
                     start=False, stop=True, reads=["emat", selk], writes=[sk])
        self.attn_flush(keep=self.LOOKAHEAD - 1)
        pf, pfk = self.Pf[bi], ("Pf", bi)
        S.act(pf[0:nrows, 0:wtot], sbank[0:nrows, 0:wtot], AF.Exp, scale=0.125, reads=[sk], writes=[pfk])
        mi = self.mk_i % 5
        self.mk_i += 1
        pm, pmk = self.Pm[mi], ("Pm", mi)
        eng = "pool" if (self.pool_share and self.mk_i % self.pool_share == 0) else "dve"
        S.tt(eng, pm[0:nrows, 0:wtot], pf[0:nrows, 0:wtot], Tt[0:nrows, tcol0:tcol0 + wtot], ALU.mult,
             reads=[pfk, Tk], writes=[pmk])
        self.chain_step()

        def pv():
            if qcols is None:
                for jj, j in enumerate(jb):
                    va, vk = vfn(j)
                    S.mm(obank[:, ocol:ocol + 65], pm[0:nrows, jj * 128:(jj + 1) * 128], va,
                         start=(first and jj == 0), stop=(last and jj == n - 1),
                         reads=[pmk, vk], writes=[ok])
            if post is not None:
                post(pm, pmk)
        self.pend.append(pv)

    def evac_copy(self, dst, key, eng="act"):
        S = self.S

        def f(s, bank, bk):
            S.cp(eng, dst[:, s * 512:(s + 1) * 512], bank, reads=[bk], writes=[key])
        return f

    def evac_qz(self, dst, key):
        S = self.S

        def f(s, bank, bk):
            S.cp("act", dst[0:64, 0, s * 512:(s + 1) * 512], bank[0:64, :], reads=[bk], writes=[key])
            S.cp("act", dst[64:128, 1, s * 512:(s + 1) * 512], bank[64:128, :], reads=[bk], writes=[key])
        return f

    def evac_silu(self, dst, key):
        S = self.S

        def f(s, bank, bk):
            S.act(dst[:, s * 512:(s + 1) * 512], bank, AF.Silu, reads=[bk], writes=[key])
        return f

    def finish_chunk(self, Otok, gT, gk, mixc, mc):
        S = self.S
        for i4 in range(4):
            bi = 7
            self.o_i += 1
            bank, bk = self.ps[bi], "ps%d" % bi
            for q in range(4):
                i = 4 * i4 + q
                S.tr(bank[:, q * 128:(q + 1) * 128], Otok[:, i, :], self.ident[:],
                     reads=[("Otok", i), "ident"], writes=[bk])
            S.tt("dve", mixc[:, i4 * 512:(i4 + 1) * 512], bank[:, 0:512], gT[:, i4 * 512:(i4 + 1) * 512], ALU.mult,
                 reads=[bk, gk], writes=["mixc"])
        self.store_chunk(mixc, mc)

    def store_chunk(self, mixc, mc):
        dst = self.mixd[:, :, mc, :].rearrange("i p t -> p i t")
        self.S.dma("sp", dst, mixc[:].rearrange("p (i t) -> p i t", i=16), "mixst", reads=["mixc"],
                   writes=[("mixd", mc)])

    def mixer_B(self, l):
        from contextlib import ExitStack
        S, ps = self.S, self.ps
        with ExitStack() as es:
            KT = self.sb(es, "b_KT", [128, 2048], BF16)
            Va = self.sb(es, "b_Va", [128, 16, 2, 65], BF16)
            esk = self.sb(es, "b_esk", [128, 8], F32)
            QT = [self.sb(es, "b_QT%d" % i, [128, 2, 2048], BF16) for i in range(2)]
            for v in range(2):
                S.memset("pool", QT[v][:], 0.0, writes=[("b_QT", v)])
            gT = self.sb(es, "b_gT", [128, 2048], F32)
            Tb = [self.sb(es, "b_T%d" % i, [128, 256], BF16) for i in range(2)]
            Otok = self.sb(es, "b_Otok", [128, 16, 128], F32)
            mixc = self.sb(es, "b_mixc", [128, 2048], BF16)
            rl = self.sb(es, "b_rl", [128, 16], F32)
            sk_b = bass.AP(self.dr["sinks"].tensor, l * 8, [[0, 128], [1, 8]])
            S.dma("sp", esk[:], sk_b, "b_sk", writes=["esk"])
            S.act(esk[:], esk[:], AF.Exp, reads=["esk"], writes=["esk"])
            self.proj_fm(l, [(SEG["bk"], 128)], self.evac_copy(KT, "b_KT"))
            S.memset("dve", Va[:, :, :, 64:65], 1.0, writes=["b_Va"])

            def ev_v(i4, bank, bk):
                S.cp("dve", Va[:, 4 * i4:4 * i4 + 4, :, 0:64], bank.rearrange("p a (g d) -> p a g d", g=2),
                     reads=[bk], writes=["b_Va"])
            self.proj_tm(l, [(SEG["bv"], 128)], ev_v)
            ti = 0
            for p in range(4):
                qt, qk = QT[p % 2], ("b_QT", p % 2)
                self.proj_fm(l, _pair_cols(SEG["bq"], p), self.evac_qz(qt, qk))
                self.proj_fm(l, _pair_cols(SEG["bgate"], p), self.evac_silu(gT, "b_gT"))
                self.attn_begin()
                for half in range(2):
                    h = p + 4 * half
                    hp = 64 * half
                    tb, tk = Tb[ti % 2], ("b_T", ti % 2)
                    ti += 1
                    S.dma("sp", tb[:], self.toep(16 + h, 0, 256), "b_T%d" % (ti % 2), reads=[("Z", 16 + h)], writes=[tk])
                    for i in range(NT):
                        oi = 4 + self.o_i % 2
                        self.o_i += 1
                        obank, ok = ps[oi], "ps%d" % oi
                        jl = [i, i - 1] if i >= 1 else [0]

                        def post(pm, pmk, i=i, h=h, hp=hp, obank=obank, ok=ok):
                            c = self.oc_i % 8
                            self.oc_i += 1
                            oc, ock = self.oc[c], ("oc", c)
                            S.cp("act", oc[:, 0:65], obank[:, 0:65], reads=[ok], writes=[ock])
                            self.chain_add([
                                lambda: S.ts("dve", rl[:, i:i + 1], oc[:, 64:65], esk[:, h:h + 1], None, ALU.add,
                                             reads=[ock, "esk"], writes=[("b_rl", i)]),
                                lambda: S.add("dve", lambda e: e.reciprocal(rl[:, i:i + 1], rl[:, i:i + 1]),
                                              reads=[("b_rl", i)], writes=[("b_rl", i)]),
                                lambda: S.ts("dve", Otok[:, i, hp:hp + 64], oc[:, 0:64], rl[:, i:i + 1], None, ALU.mult,
                                             reads=[ock, ("b_rl", i)], writes=[("Otok", i)])])
                        self.attn_batch(qt[:, half, :], qk, KT, "b_KT", hp, i, jl, tb, tk, 0,
                                        lambda j, half=half: (Va[:, j, half, :], "b_Va"),
                                        obank, ok, 0, True, True, post=post)
                self.attn_flush()
                self.finish_chunk(Otok, gT, "b_gT", mixc, 4 + p)
            self.prefetch_w(l, [(SEG["cb"], 128)])
            self.prefetch_w(l, [(SEG["cc"], 128)])
            S.fence()

    def build(self):
        from contextlib import ExitStack
        st = self.stage
        with self.es:
            with ExitStack() as es:
                self.load_consts(es)
                self.phase0_tables()
                self.hnT = self.sb(es, "hnT", [128, 16, 2048], BF16)
                self.alloc_common(es)
                for l in range(DEPTH):
                    self.phase1_norm(l, es)
                    if st == "p1":
                        break
                    if st == "W":
                        t, wk, M = self.load_w(l, _pair_cols(SEG["bq"], 3))
                        self.dump("wst", t[:], [128, 16, 128], BF16, reads=[wk])
                        break
                    if st == "B":
                        self.mixer_B(l)
                        break
                self.S.emit(self.final_keys)
        return self.nc


class ProgFull(ProgMix):
    def mixer_D(self, l):
        from contextlib import ExitStack
        S, ps = self.S, self.ps
        with ExitStack() as es:
            QT = [self.sb(es, "d_QT%d" % i, [128, 2, 2048], BF16) for i in range(2)]
            for v in range(2):
                S.memset("pool", QT[v][:], 0.0, writes=[("d_QT", v)])
            KT = [self.sb(es, "d_KT%d" % i, [128, 2048], BF16) for i in range(2)]
            Va = [self.sb(es, "d_Va%d" % i, [128, 16, 2, 65], BF16) for i in range(2)]
            gT = self.sb(es, "d_gT", [128, 2048], F32)
            Td = [self.sb(es, "d_T%d" % i, [128, 2048], BF16) for i in range(2)]
            Otok = self.sb(es, "d_Otok", [128, 16, 128], F32)
            mixc = self.sb(es, "d_mixc", [128, 2048], BF16)
            rl = self.sb(es, "d_rl", [128, 16], F32)
            for v in range(2):
                S.memset("dve", Va[v][:, :, :, 64:65], 1.0, writes=[("d_Va", v)])
            ti = 0
            for p in range(4):
                b = p % 2
                qt, qk = QT[b], ("d_QT", b)
                kt, kk = KT[b], ("d_KT", b)
                va, vk = Va[b], ("d_Va", b)
                self.proj_fm(l, _pair_cols(SEG["dq"], p), self.evac_qz(qt, qk))
                self.proj_fm(l, _pair_cols(SEG["dk"], p), self.evac_copy(kt, kk))
                self.proj_fm(l, _pair_cols(SEG["dgate"], p), self.evac_silu(gT, "d_gT"))

                def ev_v(i4, bank, bk, va=va, vk=vk):
                    S.cp("dve", va[:, 4 * i4:4 * i4 + 4, :, 0:64], bank.rearrange("p a (g d) -> p a g d", g=2),
                         reads=[bk], writes=[vk])
                self.proj_tm(l, _pair_cols(SEG["dv"], p), ev_v)
                if p == 3:
                    self.prefetch_wout(l)
                self.attn_begin()
                for half in range(2):
                    h = p + 4 * half
                    hp = 64 * half
                    td, tk = Td[ti % 2], ("d_T", ti % 2)
                    S.dma("sp", td[:], self.toep(24 + h, 0, 2048), "d_T%d" % (ti % 2), reads=[("Z", 24 + h)], writes=[tk])
                    ti += 1
                    for i in range(NT):
                        oi = 4 + self.o_i % 2
                        self.o_i += 1
                        obank, ok = ps[oi], "ps%d" % oi
                        jl = list(range(i, -1, -1))
                        nb = (len(jl) + 3) // 4
                        for bb in range(nb):
                            jb = jl[4 * bb:4 * bb + 4]
                            post = None
                            if bb == nb - 1:
                                def post(pm, pmk, i=i, hp=hp, obank=obank, ok=ok):
                                    c = self.oc_i % 8
                                    self.oc_i += 1
                                    oc, ock = self.oc[c], ("oc", c)
                                    S.cp("act", oc[:, 0:65], obank[:, 0:65], reads=[ok], writes=[ock])
                                    self.chain_add([
                                        lambda: S.add("dve", lambda e: e.reciprocal(rl[:, i:i + 1], oc[:, 64:65]),
                                                      reads=[ock], writes=[("d_rl", i)]),
                                        lambda: S.ts("dve", Otok[:, i, hp:hp + 64], oc[:, 0:64], rl[:, i:i + 1], None, ALU.mult,
                                                     reads=[ock, ("d_rl", i)], writes=[("Otok", i)])])
                            self.attn_batch(qt[:, half, :], qk, kt, kk, hp, i, jb, td, tk, 128 * (i - jb[0]),
                                            lambda j, half=half, va=va, vk=vk: (va[:, j, half, :], vk),
                                            obank, ok, 0, bb == 0, bb == nb - 1, post=post)
                self.attn_flush()
                self.finish_chunk(Otok, gT, "d_gT", mixc, 12 + p)
            S.fence()

    def mixer_C(self, l):
        from contextlib import ExitStack
        S, ps = self.S, self.ps
        with ExitStack() as es:
            cbT = self.sb(es, "c_cb", [128, 2048], F32)
            ccT = self.sb(es, "c_cc", [128, 2048], F32)
            chT = self.sb(es, "c_ch", [128, 2048], F32)
            gT = self.sb(es, "c_gT", [128, 2048], F32)
            u = self.sb(es, "c_u", [128, 2048], F32)
            y = self.sb(es, "c_y", [128, 2048], F32)
            mixc = self.sb(es, "c_mixc", [128, 2048], BF16)
            cw = self.sb(es, "c_cw", [128, 4, 3], F32)
            tmp = self.sb(es, "c_tmp", [16, 128], F32)
            for c in range(4):
                S.dma("sp", tmp[0:3, :], self.dr["conv_w"][l][:, 128 * c:128 * (c + 1)], "c_tmp", writes=["c_tmp"])
                S.tr(ps[7][:, 0:3], tmp[0:3, :], self.ident[0:3, 0:3], reads=["c_tmp", "ident"], writes=["ps7"])
                S.cp("dve", cw[:, c, :], ps[7][:, 0:3], reads=["ps7"], writes=[("c_cw", c)])
                self.proj_fm(l, [(SEG["cb"] + 128 * c, 128)], self.evac_copy(cbT, "c_cb"))
                self.proj_fm(l, [(SEG["cc"] + 128 * c, 128)], self.evac_copy(ccT, "c_cc"))
                self.proj_fm(l, [(SEG["ch"] + 128 * c, 128)], self.evac_copy(chT, "c_ch"))
                self.proj_fm(l, [(SEG["cgate"] + 128 * c, 128)], self.evac_silu(gT, "c_gT"))
                S.tt("pool", u[:], ccT[:], chT[:], ALU.mult, reads=["c_cc", "c_ch"], writes=["c_u"])
                S.ts("dve", y[:], u[:], cw[:, c, 2:3], None, ALU.mult, reads=["c_u", ("c_cw", c)], writes=["c_y"])
                S.stt(y[:, 1:2048], u[:, 0:2047], cw[:, c, 1:2], y[:, 1:2048], ALU.mult, ALU.add,
                      reads=["c_u", ("c_cw", c), "c_y"], writes=["c_y"])
                S.stt(y[:, 2:2048], u[:, 0:2046], cw[:, c, 0:1], y[:, 2:2048], ALU.mult, ALU.add,
                      reads=["c_u", ("c_cw", c), "c_y"], writes=["c_y"])
                S.tt("dve", y[:], y[:], cbT[:], ALU.mult, reads=["c_y", "c_cb"], writes=["c_y"])
                S.tt("dve", mixc[:], y[:], gT[:], ALU.mult, reads=["c_y", "c_gT"], writes=["mixc"])
                self.store_chunk(mixc, 8 + c)
            self.prefetch_w(l, _pair_cols(SEG["dq"], 0))
            self.prefetch_w(l, _pair_cols(SEG["dk"], 0))
            S.fence()

    def chunk_rows(self, mc):
        g, p = divmod(mc, 4)
        base = 512 * g
        if g == 2:
            return [(base + 128 * p, 128, 0)]
        return [(base + 64 * p, 64, 0), (base + 64 * (p + 4), 64, 64)]

    def prefetch_wout(self, l):
        S = self.S
        S.memset("pool", self.gate_t[:, 0:1], 0.0, writes=[("hnT", i) for i in range(NT)] + ["wo_gate"])
        for mc in range(16):
            for (r0, nr, p0) in self.chunk_rows(mc):
                S.dma("pool", self.hnT[p0:p0 + nr, mc, :], self.dr["w_out"][l][r0:r0 + nr, :], "o_wo%d" % (mc % 4),
                      reads=["wo_gate"], writes=[("o_wo", mc)])

    def phase3_out(self, l):
        from contextlib import ExitStack
        S, ps = self.S, self.ps
        last = (l == DEPTH - 1)
        xin = self.dr["x"] if l == 0 else self.xres
        with ExitStack() as es:
            wo = self.hnT
            mt = [self.sb(es, "o_mt%d" % i, [128, 16, 128], BF16) for i in range(2)]
            xt = [self.sb(es, "o_xt%d" % i, [128, 2048], F32) for i in range(2)]
            xo = [self.sb(es, "o_xo%d" % i, [128, 2048], F32) for i in range(2)]
            if last:
                fnw = self.sb(es, "o_fnw", [128, 2048], F32)
                ss = self.sb(es, "o_ss", [128, 16], F32)
                rs = self.sb(es, "o_rs", [128, 16], F32)
                rstd = self.sb(es, "o_rstd", [128, 16], F32)
                fb = bass.AP(self.dr["final_norm_w"].tensor, 0, [[0, 128], [1, 2048]])
                S.dma("sp", fnw[:], fb, "o_fnw", writes=["o_fnw"])
            def loads(i):
                b = i % 2
                S.dma("sp", mt[b][:, 0:12, :], self.mixd[i][:, 0:12, :], "o_mt%d" % b,
                      reads=[("mixd", mc) for mc in range(12)], writes=[("o_mt", b, 0)])
                S.dma("sp", mt[b][:, 12:16, :], self.mixd[i][:, 12:16, :], "o_mtb%d" % b,
                      reads=[("mixd", mc) for mc in range(12, 16)], writes=[("o_mt", b, 1)])
                S.dma("sp", xt[b][:], xin[i * 128:(i + 1) * 128, :], "o_xt%d" % b, reads=[("xres", i)],
                      writes=[("o_xt", b)])

            loads(0)
            for i in range(NT):
                b = i % 2
                if i + 1 < NT:
                    loads(i + 1)
                for s in range(4):
                    bi = 6 + self.pj_i % 2
                    self.pj_i += 1
                    bank, bk = ps[bi], "ps%d" % bi
                    for mc in range(16):
                        S.mm(bank[:, 0:512], mt[b][:, mc, :], wo[:, mc, s * 512:(s + 1) * 512],
                             start=(mc == 0), stop=(mc == 15), reads=[("o_mt", b, 0 if mc < 12 else 1), ("o_wo", mc)], writes=[bk])
                    S.tt("dve", xo[b][:, s * 512:(s + 1) * 512], bank[:, 0:512], xt[b][:, s * 512:(s + 1) * 512], ALU.add,
                         reads=[bk, ("o_xt", b)], writes=[("o_xo", b)])
                if not last:
                    S.dma("sp", self.xres[i * 128:(i + 1) * 128, :], xo[b][:], "o_st%d" % b, reads=[("o_xo", b)],
                          writes=[("xres", i)])
                else:
                    S.act(xt[b][:], xo[b][:], AF.Square, accum_out=ss[:, i:i + 1],
                          reads=[("o_xo", b)], writes=[("o_xt", b), ("o_ss", i)])
                    S.act(rs[:, i:i + 1], ss[:, i:i + 1], AF.Sqrt, scale=1.0 / D_MODEL, bias=self.eps_t[:, 0:1],
                          reads=[("o_ss", i), "eps"], writes=[("o_rs", i)])
                    S.add("dve", lambda e, i=i: e.reciprocal(rstd[:, i:i + 1], rs[:, i:i + 1]),
                          reads=[("o_rs", i)], writes=[("o_rstd", i)])
                    S.stt(xo[b][:], xo[b][:], rstd[:, i:i + 1], fnw[:], ALU.mult, ALU.mult,
                          reads=[("o_xo", b), ("o_rstd", i), "o_fnw"], writes=[("o_xo", b)])
                    S.dma("sp", self.out[i * 128:(i + 1) * 128, :], xo[b][:], "o_st%d" % b, reads=[("o_xo", b)],
                          writes=[("out", i)])
                    if "o_st%d" % b not in self.final_keys:
                        self.final_keys.append("o_st%d" % b)
            S.fence()
        if not last:
            self.dump("xres", self.xres, [SEQ, D_MODEL], F32, reads=[("xres", i) for i in range(NT)])

    def build(self):
        from contextlib import ExitStack
        st = self.stage
        with self.es:
            with ExitStack() as es:
                self.load_consts(es)
                self.phase0_extra(es)
                self.hnT = self.sb(es, "hnT", [128, 16, 2048], BF16)
                self.alloc_common(es)
                for l in range(DEPTH):
                    if l == 0:
                        with ExitStack() as es0:
                            self.phase0_tables(es0)
                            self.phase1_norm(l, es)
                    else:
                        self.phase1_norm(l, es)
                    if st in ("full", "A") or "A" in st:
                        self.mixer_A(l)
                    if st in ("full",) or "B" in st:
                        self.mixer_B(l)
                    if st in ("full",) or "C" in st:
                        self.mixer_C(l)
                    if st in ("full",) or "D" in st:
                        self.mixer_D(l)
                    if st in ("full",) or "O" in st:
                        self.phase3_out(l)
                    if st != "full":
                        self.dump("mixd", self.mixd, [16, 128, 16, 128], BF16, reads=[("mixd", c) for c in range(16)])
                        break
                self.S.emit(self.final_keys)
        return self.nc


WC = 4224


class ProgA(ProgFull):
    def phase0_extra(self, es):
        S, dr = self.S, self.dr
        self.ZCh = self.nc.dram_tensor("zctab", [8, 128, WC], BF16, kind="Internal")
        self.emat = self.sb(es, "emat", [128, 16, 128], BF16)
        S.memset("pool", self.emat[:], 0.0, writes=["emat"])
        self.ovl1b = self.sb(es, "ovl1b", [128, 33], BF16)
        self.amask = self.sb(es, "amask", [128, 16, 32], F32)
        self.bmask = self.sb(es, "bmask", [128, 16, 32], F32)
        S.dma("pool", self.emat[0:32, :, :], dr["emat"], "c_emat", writes=["emat"])
        S.dma("pool", self.ovl1b[0:127, :], dr["ovl1"], "c_ovl", writes=["ovl1b"])
        S.dma("sp", self.amask[:], dr["amask"], "c_am", writes=["amask"])
        S.dma("sp", self.bmask[:], dr["bmask"], "c_bm", writes=["bmask"])

    def toepC(self, h):
        return bass.AP(self.ZCh, h * 128 * WC + 2016, [[WC - 16, 128], [1, 2048]])

    def phase0_tables(self, es_outer=None):
        from contextlib import ExitStack
        with ExitStack() as es_inner:
            es = es_outer if es_outer is not None else es_inner
            z = self.sb(es, "zc_zero", [128, 1920], BF16)
            self.S.memset("dve", z[:], 0.0, writes=["zc_zero"])
            for h in range(8):
                self.S.dma("sp", self.ZCh.ap()[h][:, 0:1920], z[:], "zc_z", reads=["zc_zero"], writes=[("ZCz", h)])
            self._zc_hook = True
            super().phase0_tables(es_outer)

    def mixer_A(self, l):
        from contextlib import ExitStack
        S, ps, dr = self.S, self.ps, self.dr
        with ExitStack() as es:
            QT = self.sb(es, "a_QT", [128, 4, 2, 2048], BF16)
            S.memset("pool", QT[:], 0.0, writes=[("a_QT", p) for p in range(4)])
            KST = self.sb(es, "a_KST", [128, 2048], BF16)
            KWT = self.sb(es, "a_KWT", [128, 2048], BF16)
            VS = self.sb(es, "a_VS", [128, 16, 2, 65], BF16)
            VW = self.sb(es, "a_VW", [128, 16, 2, 65], BF16)
            gsig = self.sb(es, "a_gsig", [128, 16, 24], F32)
            kcT = self.sb(es, "a_kcT", [128, 128], BF16)
            vca = self.sb(es, "a_vca", [128, 2, 65], BF16)
            impacc = self.sb(es, "a_imp", [128, 16, 2, 32], F32)
            selT = [self.sb(es, "a_selT%d" % g, [128, 2048], BF16) for g in range(2)]
            for g in range(2):
                S.memset("pool", selT[g][:], 0.0, writes=[("a_selT", g)])
            with ExitStack() as es1:
                KCT = self.sb(es1, "a_KCT", [128, 2048], BF16)
                VCT = self.sb(es1, "a_VCT", [128, 2048], BF16)
                w1 = self.sb(es1, "a_w1", [128, 2, 32, 128], BF16)
                w2 = self.sb(es1, "a_w2", [128, 2, 64], BF16)
                ptmp = self.sb(es1, "a_ptmp", [32, 128], F32)
                posT = self.sb(es1, "a_posT", [64, 2, 32], BF16)
                hb = self.sb(es1, "a_hb", [128, 2], F32)
                hs = self.sb(es1, "a_hs", [128, 2, 2, 128], BF16)
                for kind in range(2):
                    for half in range(2):
                        S.dma("pool", w1[64 * half:64 * half + 64, kind, :, :],
                              dr["cmp_w1"][l, kind].rearrange("(l d) n -> d l n", d=64),
                              "a_w1_%d%d" % (kind, half), writes=["a_w1"])
                    S.dma("pool", w2[:, kind, :], dr["cmp_w2"][l, kind], "a_w2_%d" % kind, writes=["a_w2"])
                    S.dma("sp", ptmp[:, kind * 64:(kind + 1) * 64], dr["cmp_pos"][l, kind], "a_pos%d" % kind,
                          writes=["a_ptmp"])
                for kind in range(2):
                    S.tr(ps[7][0:64, 32 * kind:32 * kind + 32], ptmp[:, kind * 64:(kind + 1) * 64], self.ident[0:32, 0:32],
                         reads=["a_ptmp", "ident"], writes=["ps7"])
                    S.cp("dve", posT[:, kind, :], ps[7][0:64, 32 * kind:32 * kind + 32], reads=["ps7"], writes=["a_posT"])
                self.proj_fm(l, [(SEG["akc"], 128)], self.evac_copy(KCT, "a_KCT"))
                self.proj_fm(l, [(SEG["avc"], 128)], self.evac_copy(VCT, "a_VCT"))
                for p in range(4):
                    def ev_q(s, bank, bk, p=p):
                        S.cp("act", QT[0:64, p, 0, s * 512:(s + 1) * 512], bank[0:64, :], reads=[bk], writes=[("a_QT", p)])
                        S.cp("act", QT[64:128, p, 1, s * 512:(s + 1) * 512], bank[64:128, :], reads=[bk], writes=[("a_QT", p)])
                    self.proj_fm(l, _pair_cols(SEG["aq"], p), ev_q)
                for kind, XT, xk in ((0, KCT, "a_KCT"), (1, VCT, "a_VCT")):
                    for l_ in range(32):
                        S.mm(ps[5][:, kind:kind + 1], w1[0:64, kind, l_, :], posT[0:64, kind, l_:l_ + 1],
                             start=(l_ == 0), stop=(l_ == 31), reads=["a_w1", "a_posT"], writes=["ps5"])
                    S.cp("dve", hb[:, kind:kind + 1], ps[5][:, kind:kind + 1], reads=["ps5"], writes=[("a_hb", kind)])
                    for g in range(2):
                        hp = 64 * g
                        bi = 6 + self.pj_i % 2
                        self.pj_i += 1
                        bank, bk = ps[bi], "ps%d" % bi
                        for l_ in range(32):
                            S.mm(bank[:, 0:127], w1[hp:hp + 64, kind, l_, :], XT[hp:hp + 64, l_:l_ + 2017:16],
                                 start=(l_ == 0), stop=(l_ == 31), reads=["a_w1", xk], writes=[bk])
                        S.act(hs[:, kind, g, 0:127], bank[:, 0:127], AF.Silu, bias=hb[:, kind:kind + 1],
                              reads=[bk, ("a_hb", kind)], writes=["a_hs"])
                for g in range(2):
                    S.mm(ps[7][64 * g:64 * g + 64, 0:127], w2[:, 0, :], hs[:, 0, g, 0:127],
                         reads=["a_w2", "a_hs"], writes=["ps7"])
                    S.cp("dve", kcT[64 * g:64 * g + 64, 0:127], ps[7][64 * g:64 * g + 64, 0:127], reads=["ps7"], writes=["a_kcT"])
                S.memset("dve", vca[:, :, 64:65], 1.0, writes=["a_vca"])
                for g in range(2):
                    S.mm(ps[4][0:127, 64 * g:64 * g + 64], hs[:, 1, g, 0:127], w2[:, 1, :],
                         reads=["a_w2", "a_hs"], writes=["ps4"])
                    S.cp("dve", vca[0:127, g, 0:64], ps[4][0:127, 64 * g:64 * g + 64], reads=["ps4"], writes=["a_vca"])
                self.prefetch_w(l, [(SEG["aks"], 128)])
                self.prefetch_w(l, [(SEG["akw"], 128)])
                S.fence()
            self.dump("kcT", kcT[:], [128, 128], BF16, reads=["a_kcT"])
            self.dump("vca", vca[:], [128, 2, 65], BF16, reads=["a_vca"])
            with ExitStack() as es3:
                Tc = [self.sb(es, "a_Tc1_%d" % i, [128, 2048], BF16) for i in range(2)]
                rl4 = [self.sb(es, "a_rl4_%d" % i, [128, 4], F32) for i in range(2)]
                S.memset("pool", impacc[:], 0.0, writes=[("a_imp", i, g) for i in range(NT) for g in range(2)])
                ti = 0
                ri = 0
                self.attn_begin()
                for p in range(4):
                    for half in range(2):
                        h = p + 4 * half
                        hp = 64 * half
                        tc, tck = Tc[ti % 2], ("a_Tc", ti % 2)
                        S.dma("sp", tc[:, :], self.toepC(h), "a_Tc%d" % (ti % 2),
                              reads=[("ZC", h), ("ZCz", h)], writes=[tck])
                        ti += 1
                        for s in range(4):
                            def post(pm, pmk, s=s, half=half):
                                nonlocal ri
                                oi = 4 + self.o_i % 2
                                self.o_i += 1
                                ub, uk = ps[oi], "ps%d" % oi
                                for qq in range(4):
                                    S.mm(ub[:, 33 * qq:33 * qq + 33], pm[0:127, qq * 128:(qq + 1) * 128], self.ovl1b[0:127, :],
                                         reads=[pmk, "ovl1b"], writes=[uk])
                                r4, rk = rl4[ri % 2], ("a_rl4", ri % 2)
                                ri += 1
                                S.ts("dve", r4[:, 0:4], ub[:, 0:132:33], 1e-30, None, ALU.max, reads=[uk], writes=[rk])
                                S.add("dve", lambda e, r4=r4: e.reciprocal(r4[:, 0:4], r4[:, 0:4]), reads=[rk], writes=[rk])
                                for qq in range(4):
                                    i = 4 * s + qq
                                    S.stt(impacc[:, i, half, :], ub[:, 33 * qq + 1:33 * qq + 33], r4[:, qq:qq + 1],
                                          impacc[:, i, half, :], ALU.mult, ALU.add,
                                          reads=[uk, rk, ("a_imp", i, half)], writes=[("a_imp", i, half)])
                            self.attn_batch(None, ("a_QT", p), None, "a_kcT", hp, 0, [0], tc, tck, s * 512,
                                            None, None, None, 0, True, True, post=post, nrows=127,
                                            qcols=QT[:, p, half, s * 512:(s + 1) * 512], kcols=kcT[:, 0:127])
                self.attn_flush()
            self.proj_fm(l, [(SEG["aks"], 128)], self.evac_copy(KST, "a_KST"))
            self.proj_fm(l, [(SEG["akw"], 128)], self.evac_copy(KWT, "a_KWT"))
            for V, vk, seg in ((VS, "a_VS", "avs"), (VW, "a_VW", "avw")):
                S.memset("pool", V[:, :, :, 64:65], 1.0, writes=[vk])

                def ev_v(i4, bank, bk, V=V, vk=vk):
                    S.cp("act", V[:, 4 * i4:4 * i4 + 4, :, 0:64], bank.rearrange("p a (g d) -> p a g d", g=2),
                         reads=[bk], writes=[vk])
                self.proj_tm(l, [(SEG[seg], 128)], ev_v)

            def ev_g(i4, bank, bk):
                S.act(gsig[:, 4 * i4:4 * i4 + 4, :], bank, AF.Sigmoid, reads=[bk], writes=["a_gsig"])
            self.proj_tm(l, [(SEG["agates"], 24)], ev_g)
            self.dump("imp", impacc[:], [128, 16, 2, 32], F32, reads=[("a_imp", i, g) for i in range(NT) for g in range(2)])
            with ExitStack() as es4:
                impf = [self.sb(es, "a_impf%d" % i, [128, 32], F32) for i in range(2)]
                top8 = [self.sb(es, "a_top8_%d" % i, [128, 8], F32) for i in range(2)]
                seln = [self.sb(es, "a_seln%d" % i, [128, 32], BF16) for i in range(2)]
                ci = 0
                for g in range(2):
                    for i4 in range(4):
                        bi = 7
                        self.o_i += 1
                        bank, bk = ps[bi], "ps%d" % bi
                        for q in range(4):
                            i = 4 * i4 + q
                            c = ci % 2
                            ci += 1
                            S.tt("dve", impf[c][:], impacc[:, i, g, :], self.amask[:, i, :], ALU.mult,
                                 reads=[("a_imp", i, g), "amask"], writes=[("a_impf", c)])
                            S.tt("dve", impf[c][:], impf[c][:], self.bmask[:, i, :], ALU.add,
                                 reads=[("a_impf", c), "bmask"], writes=[("a_impf", c)])
                            S.add("dve", lambda e, c=c: e.max(top8[c][:], impf[c][:]),
                                  reads=[("a_impf", c)], writes=[("a_top8", c)])
                            S.ts("dve", seln[c][:], impf[c][:], top8[c][:, 7:8], 1.0, ALU.is_ge, ALU.subtract,
                                 reads=[("a_impf", c), ("a_top8", c)], writes=[("a_seln", c)])
                            S.mm(bank[0:32, q * 128:(q + 1) * 128], seln[c][:], self.identb[:],
                                 reads=[("a_seln", c), "identb"], writes=[bk])
                        S.cp("act", selT[g][0:32, i4 * 512:(i4 + 1) * 512], bank[0:32, 0:512], reads=[bk], writes=[("a_selT", g)])
                self.prefetch_w(l, _pair_cols(SEG["agate"], 0))
            self.dump("selT", selT[0][0:32, :], [32, 2048], BF16, reads=[("a_selT", 0)])
            self.pool_share = 0
            with ExitStack() as es5:
                Tsl = [self.sb(es5, "a_Tsl%d" % i, [128, 2048], BF16) for i in range(2)]
                Twn = [self.sb(es5, "a_Twn%d" % i, [128, 640], BF16) for i in range(2)]
                PmC = [self.sb(es5, "a_PmC%d" % i, [128, 512], BF16) for i in range(2)]
                gT = self.sb(es5, "a_gT", [128, 2048], F32)
                Otok = self.sb(es5, "a_Otok", [128, 16, 128], F32)
                mixc = self.sb(es5, "a_mixc", [128, 2048], BF16)
                rl3 = [self.sb(es5, "a_rl3_%d" % i, [128, 3], F32) for i in range(8)]
                s3 = [self.sb(es5, "a_s3_%d" % i, [128, 3], F32) for i in range(8)]
                ot1 = [self.sb(es5, "a_ot1_%d" % i, [128, 64], F32) for i in range(8)]
                ot2 = [self.sb(es5, "a_ot2_%d" % i, [128, 64], F32) for i in range(8)]
                ti = 0
                ci = 0
                pci = 0
                for p in range(4):
                    self.proj_fm(l, _pair_cols(SEG["agate"], p), self.evac_silu(gT, "a_gT"))
                    self.attn_begin()
                    for half in range(2):
                        h = p + 4 * half
                        hp = 64 * half
                        tb = ti % 2
                        ti += 1
                        tsl, tslk = Tsl[tb], ("a_Tsl", tb)
                        twn, twnk = Twn[tb], ("a_Twn", tb)
                        tc, tck = Tc[tb], ("a_Tc", tb)
                        S.dma("sp", tsl[:], self.toep(h, 0, 2048), "a_Tsl%d" % tb, reads=[("Z", h)], writes=[tslk])
                        S.dma("sp", twn[:], self.toep(8 + h, 0, 640), "a_Twn%d" % tb, reads=[("Z", 8 + h)], writes=[twnk])
                        S.dma("sp", tc[:, :], self.toepC(h), "a_Tc%d" % tb, reads=[("ZC", h), ("ZCz", h)], writes=[tck])
                        pmc = None
                        for i in range(NT):
                            oi = 4 + self.o_i % 2
                            self.o_i += 1
                            obank, ok = ps[oi], "ps%d" % oi
                            if i % 4 == 0:
                                s = i // 4
                                pmc, pmck = PmC[pci % 2], ("a_PmC", pci % 2)
                                pci += 1

                                def postc(pm, pmk, pmc=pmc, pmck=pmck):
                                    S.cp("pool", pmc[0:127, :], pm[0:127, 0:512], reads=[pmk], writes=[pmck])
                                self.attn_batch(None, ("a_QT", p), None, "a_kcT", hp, 0, [0], tc, tck, s * 512,
                                                None, None, None, 0, True, True, post=postc, nrows=127,
                                                qcols=QT[:, p, half, s * 512:(s + 1) * 512],
                                                kcols=kcT[:, 0:127])
                            jl = list(range(i, -1, -1))
                            nb = (len(jl) + 3) // 4
                            for bb in range(nb):
                                jb = jl[4 * bb:4 * bb + 4]
                                self.attn_batch(QT[:, p, half, :], ("a_QT", p), KST, "a_KST", hp, i, jb, tsl, tslk,
                                                128 * (i - jb[0]),
                                                lambda j, half=half: (VS[:, j, half, :], "a_VS"),
                                                obank, ok, 65, bb == 0, bb == nb - 1,
                                                selT=selT[half], selk=("a_selT", half))
                            jl = list(range(i, max(0, i - 4) - 1, -1))
                            nb = (len(jl) + 3) // 4
                            for bb in range(nb):
                                jb = jl[4 * bb:4 * bb + 4]
                                post = None
                                if bb == nb - 1:
                                    c = ci % 8
                                    ci += 1

                                    def post(pm, pmk, i=i, h=h, hp=hp, half=half, obank=obank, ok=ok, c=c, pmc=pmc, pmck=pmck):
                                        S.mm(obank[:, 0:65], pmc[0:127, (i % 4) * 128:(i % 4 + 1) * 128], vca[0:127, half, :],
                                             reads=[pmck, "a_vca"], writes=[ok])
                                        cc = self.oc_i % 8
                                        self.oc_i += 1
                                        oc, ock = self.oc[cc], ("oc", cc)
                                        S.cp("act", oc[:, 0:195], obank[:, 0:195], reads=[ok], writes=[ock])
                                        self.chain_add([
                                            lambda: S.ts("dve", rl3[c][:], oc[:, 64:195:65], 1e-30, None, ALU.max,
                                                         reads=[ock], writes=[("a_rl3", c)]),
                                            lambda: S.add("dve", lambda e: e.reciprocal(rl3[c][:], rl3[c][:]),
                                                          reads=[("a_rl3", c)], writes=[("a_rl3", c)]),
                                            lambda: S.tt("dve", s3[c][:], rl3[c][:], gsig[:, i, h:24:8], ALU.mult,
                                                         reads=[("a_rl3", c), "a_gsig"], writes=[("a_s3", c)]),
                                            lambda: S.ts("dve", ot1[c][:], oc[:, 0:64], s3[c][:, 0:1], None, ALU.mult,
                                                         reads=[ock, ("a_s3", c)], writes=[("a_ot1", c)]),
                                            lambda: S.stt(ot2[c][:], oc[:, 65:129], s3[c][:, 1:2], ot1[c][:], ALU.mult, ALU.add,
                                                          reads=[ock, ("a_s3", c), ("a_ot1", c)], writes=[("a_ot2", c)]),
                                            lambda: S.stt(Otok[:, i, hp:hp + 64], oc[:, 130:194], s3[c][:, 2:3], ot2[c][:],
                                                          ALU.mult, ALU.add,
                                                          reads=[ock, ("a_s3", c), ("a_ot2", c)], writes=[("Otok", i)])])
                                self.attn_batch(QT[:, p, half, :], ("a_QT", p), KWT, "a_KWT", hp, i, jb, twn, twnk,
                                                128 * (i - jb[0]),
                                                lambda j, half=half: (VW[:, j, half, :], "a_VW"),
                                                obank, ok, 130, bb == 0, bb == nb - 1, post=post)
                    self.attn_flush()
                    self.finish_chunk(Otok, gT, "a_gT", mixc, p)
                S.fence()
            self.pool_share = 0
            self.prefetch_w(l, [(SEG["bk"], 128)])
            self.prefetch_w(l, [(SEG["bv"], 128)])
            S.fence()
```
